# Optimizing a Trainium2 kernel written in Bass

```python
import math
import jax, jax.numpy as jnp
from jax import lax
import numpy as np

D_MODEL = 1024
BATCH = 8
SEQ = 4096
DEPTH = 4

HEAD_DIM = 64
N_META = 16
BLOCK = 128
NEG_INF = -1e30
SWA_WINDOW = 128
SWA_Q_HEADS = 8
SWA_KV_HEADS = 2
SWA_GROUP = SWA_Q_HEADS // SWA_KV_HEADS
FOX_HEADS = 8
LRU_WIDTH = D_MODEL // 2
LRU_BLOCKS = 8
LRU_BLOCK_DIM = LRU_WIDTH // LRU_BLOCKS
CONV_WIDTH = 4
LRU_C = 8.0
REL_BUCKETS = 32
REL_MAX_DIST = 128
D_FF = ((-(-8 * D_MODEL // 3)) + 255) // 256 * 256
N_BRANCH = 3
SPLIT_SIZES = (
    SWA_Q_HEADS * HEAD_DIM,
    SWA_KV_HEADS * HEAD_DIM,
    SWA_KV_HEADS * HEAD_DIM,
    FOX_HEADS * HEAD_DIM,
    FOX_HEADS * HEAD_DIM,
    FOX_HEADS * HEAD_DIM,
    FOX_HEADS,
    LRU_WIDTH,
    LRU_WIDTH,
    N_BRANCH * D_MODEL,
)
IN_COLS = sum(SPLIT_SIZES)

kernel_name = "hybrid_swa_fox_rglru_block"


def rms_norm(x, g, eps=1e-6):
    xf = x.astype(jnp.float32)
    y = xf * lax.rsqrt(jnp.mean(xf * xf, axis=-1, keepdims=True) + eps)
    return (y * g.astype(jnp.float32)).astype(x.dtype)


def pad_left(a, n):
    return jnp.pad(a, [(0, 0), (n, 0)] + [(0, 0)] * (a.ndim - 2))


def t5_bucket(dist):
    max_exact = REL_BUCKETS // 2
    d = jnp.maximum(dist, 0)
    scaled = jnp.log(jnp.maximum(d, 1).astype(jnp.float32) / max_exact) / math.log(REL_MAX_DIST / max_exact)
    large = jnp.minimum(max_exact + (scaled * (REL_BUCKETS - max_exact)).astype(jnp.int32), REL_BUCKETS - 1)
    return jnp.where(d < max_exact, d, large)


def swa_sink_attention(q, k, v, sinks, rel_table, n_pad):
    b, tp, _, dh = q.shape
    nb = tp // BLOCK
    qb = q.reshape(b, nb, BLOCK, SWA_KV_HEADS, SWA_GROUP, dh)

    def band(a):
        a = a.reshape(b, nb, BLOCK, SWA_KV_HEADS, dh)
        prev = jnp.pad(a, ((0, 0), (1, 0), (0, 0), (0, 0), (0, 0)))[:, :-1]
        return jnp.concatenate([prev, a], axis=2)

    k_band, v_band = band(k), band(v)
    s = jnp.einsum('bnqhgd,bnkhd->bnhgqk', qb, k_band).astype(jnp.float32) * (dh ** -0.5)
    q_idx = jnp.arange(BLOCK)[:, None]
    k_idx = jnp.arange(2 * BLOCK)[None, :]
    dist = q_idx + BLOCK - k_idx
    bias = rel_table.astype(jnp.float32)[t5_bucket(dist)]
    bias = bias.transpose(2, 0, 1).reshape(SWA_KV_HEADS, SWA_GROUP, BLOCK, 2 * BLOCK)
    key_abs = (jnp.arange(nb)[:, None] - 1) * BLOCK + k_idx
    mask = ((dist >= 0) & (dist < SWA_WINDOW))[None] & (key_abs >= n_pad)[:, None, :]
    s = jnp.where(mask[None, :, None, None], s + bias, NEG_INF)
    sink = sinks.astype(jnp.float32).reshape(SWA_KV_HEADS, SWA_GROUP)[None, None, :, :, None, None]
    m = jnp.maximum(jnp.max(s, axis=-1, keepdims=True), sink)
    p = jnp.exp(s - m)
    denom = jnp.sum(p, axis=-1, keepdims=True) + jnp.exp(sink - m)
    p = (p / denom).astype(v.dtype)
    o = jnp.einsum('bnhgqk,bnkhd->bnqhgd', p, v_band)
    return o.reshape(b, tp, SWA_Q_HEADS * dh)


def forgetting_attention(q, k, v, log_f, n_pad):
    b, tp, h, dh = q.shape
    nb = tp // BLOCK
    cum = jnp.cumsum(log_f, axis=1).transpose(0, 2, 1)
    k_pos = jnp.arange(tp)
    qb = q.reshape(b, nb, BLOCK, h, dh).transpose(1, 0, 2, 3, 4)
    cb = cum.reshape(b, h, nb, BLOCK).transpose(2, 0, 1, 3)

    def one_block(args):
        qi, ci, n = args
        s = jnp.einsum('bqhd,bkhd->bhqk', qi, k).astype(jnp.float32) * (dh ** -0.5)
        s = s + ci[..., :, None] - cum[:, :, None, :]
        q_pos = n * BLOCK + jnp.arange(BLOCK)
        mask = (k_pos[None, :] <= q_pos[:, None]) & (k_pos >= n_pad)[None, :]
        s = jnp.where(mask, s, NEG_INF)
        p = jax.nn.softmax(s, axis=-1).astype(v.dtype)
        return jnp.einsum('bhqk,bkhd->bqhd', p, v)

    o = lax.map(one_block, (qb, cb, jnp.arange(nb)))
    return o.transpose(1, 0, 2, 3, 4).reshape(b, tp, h * dh)


def causal_depthwise_conv(x, w, bias):
    t = x.shape[1]
    xp = jnp.pad(x, ((0, 0), (CONV_WIDTH - 1, 0), (0, 0)))
    out = xp[:, 0:t] * w[0]
    for i in range(1, CONV_WIDTH):
        out = out + xp[:, i:i + t] * w[i]
    return out + bias


def rg_lru(x, w_r, b_r, w_i, b_i, lam):
    b, t, c = x.shape
    xb = x.reshape(b, t, LRU_BLOCKS, LRU_BLOCK_DIM)
    r = jax.nn.sigmoid(jnp.einsum('bthi,hij->bthj', xb, w_r).reshape(b, t, c).astype(jnp.float32) + b_r)
    gi = jax.nn.sigmoid(jnp.einsum('bthi,hij->bthj', xb, w_i).reshape(b, t, c).astype(jnp.float32) + b_i)
    log_a = LRU_C * r * jax.nn.log_sigmoid(lam.astype(jnp.float32))
    a = jnp.exp(log_a)
    inp = jnp.sqrt(-jnp.expm1(2.0 * log_a)) * (gi * x.astype(jnp.float32))

    def combine(left, right):
        a1, b1 = left
        a2, b2 = right
        return a1 * a2, a2 * b1 + b2

    _, h = lax.associative_scan(combine, (a, inp), axis=1)
    return h.astype(x.dtype)


def setup_inputs(seed: int = 0) -> dict:
    key = jax.random.key(seed)
    ks = jax.random.split(key, 24)
    f32 = jnp.float32
    nrm = lambda k, shape, scale: jax.random.normal(k, shape, f32) * scale
    u = jax.random.uniform(ks[10], (DEPTH, LRU_WIDTH), f32, 0.9, 0.999)
    a0 = u ** (1.0 / LRU_C)
    return {
        "x": nrm(ks[0], (BATCH, SEQ, D_MODEL), 1.0),
        "meta_tokens": nrm(ks[1], (N_META, D_MODEL), 1.0),
        "rel_bias_table": nrm(ks[2], (REL_BUCKETS, SWA_Q_HEADS), 0.5),
        "norm_mix": 1.0 + nrm(ks[3], (DEPTH, D_MODEL), 0.02),
        "w_in": nrm(ks[4], (DEPTH, D_MODEL, IN_COLS), D_MODEL ** -0.5),
        "swa_sinks": nrm(ks[5], (DEPTH, SWA_Q_HEADS), 0.5),
        "fox_forget_bias": 2.0 + 3.0 * jax.random.uniform(ks[6], (DEPTH, FOX_HEADS), f32),
        "conv_w": nrm(ks[7], (DEPTH, CONV_WIDTH, LRU_WIDTH), CONV_WIDTH ** -0.5),
        "conv_b": nrm(ks[8], (DEPTH, LRU_WIDTH), 0.02),
        "lru_w_r": nrm(ks[9], (DEPTH, LRU_BLOCKS, LRU_BLOCK_DIM, LRU_BLOCK_DIM), LRU_BLOCK_DIM ** -0.5),
        "lru_b_r": nrm(ks[11], (DEPTH, LRU_WIDTH), 0.02),
        "lru_w_i": nrm(ks[12], (DEPTH, LRU_BLOCKS, LRU_BLOCK_DIM, LRU_BLOCK_DIM), LRU_BLOCK_DIM ** -0.5),
        "lru_b_i": nrm(ks[13], (DEPTH, LRU_WIDTH), 0.02),
        "lru_lambda": jnp.log(a0) - jnp.log1p(-a0),
        "w_branch": nrm(ks[14], (DEPTH, N_BRANCH, LRU_WIDTH, D_MODEL), LRU_WIDTH ** -0.5),
        "w_out": nrm(ks[15], (DEPTH, D_MODEL, D_MODEL), D_MODEL ** -0.5),
        "norm_ffn": 1.0 + nrm(ks[16], (DEPTH, D_MODEL), 0.02),
        "w_ffn_in": nrm(ks[17], (DEPTH, D_MODEL, 2 * D_FF), D_MODEL ** -0.5),
        "w_ffn_out": nrm(ks[18], (DEPTH, D_FF, D_MODEL), D_FF ** -0.5),
        "norm_final": 1.0 + nrm(ks[19], (D_MODEL,), 0.02),
    }


def reference(x, meta_tokens, rel_bias_table, norm_mix, w_in, swa_sinks, fox_forget_bias, conv_w, conv_b,
              lru_w_r, lru_b_r, lru_w_i, lru_b_i, lru_lambda, w_branch, w_out, norm_ffn, w_ffn_in,
              w_ffn_out, norm_final):
    b = x.shape[0]
    meta = jnp.broadcast_to(meta_tokens.astype(x.dtype)[None], (b, N_META, D_MODEL))
    h = jnp.concatenate([meta, x], axis=1)
    t = h.shape[1]
    n_pad = (-t) % BLOCK
    split_points = [int(p) for p in np.cumsum(SPLIT_SIZES)[:-1]]
    for l in range(DEPTH):
        u = rms_norm(h, norm_mix[l])
        proj = u @ w_in[l]
        qa, ka, va, qf, kf, vf, fl, xc, yc, gates = jnp.split(proj, split_points, axis=-1)
        o_a = swa_sink_attention(
            pad_left(qa.reshape(b, t, SWA_Q_HEADS, HEAD_DIM), n_pad),
            pad_left(ka.reshape(b, t, SWA_KV_HEADS, HEAD_DIM), n_pad),
            pad_left(va.reshape(b, t, SWA_KV_HEADS, HEAD_DIM), n_pad),
            swa_sinks[l], rel_bias_table, n_pad)[:, n_pad:]
        log_f = jax.nn.log_sigmoid(fl.astype(jnp.float32) + fox_forget_bias[l].astype(jnp.float32))
        o_f = forgetting_attention(
            pad_left(qf.reshape(b, t, FOX_HEADS, HEAD_DIM), n_pad),
            pad_left(kf.reshape(b, t, FOX_HEADS, HEAD_DIM), n_pad),
            pad_left(vf.reshape(b, t, FOX_HEADS, HEAD_DIM), n_pad),
            pad_left(log_f, n_pad), n_pad)[:, n_pad:]
        xc = causal_depthwise_conv(xc, conv_w[l], conv_b[l])
        o_c = rg_lru(xc, lru_w_r[l], lru_b_r[l], lru_w_i[l], lru_b_i[l], lru_lambda[l]) * jax.nn.gelu(yc)
        g = jax.nn.sigmoid(gates).reshape(b, t, N_BRANCH, D_MODEL)
        merged = (g[:, :, 0] * (o_a @ w_branch[l, 0])
                  + g[:, :, 1] * (o_f @ w_branch[l, 1])
                  + g[:, :, 2] * (o_c @ w_branch[l, 2]))
        h = h + merged @ w_out[l]
        u = rms_norm(h, norm_ffn[l])
        gate_ff, up_ff = jnp.split(u @ w_ffn_in[l], 2, axis=-1)
        h = h + (jax.nn.silu(gate_ff) * up_ff) @ w_ffn_out[l]
    return rms_norm(h, norm_final)[:, N_META:]
```

```python
import numpy as np
import concourse.bass as bass
import concourse.mybir as mybir
from concourse.bass_utils import run_bass_kernel_spmd

F32 = mybir.dt.float32
BF16 = mybir.dt.bfloat16
AF = mybir.ActivationFunctionType
ALU = mybir.AluOpType

D = 1024
KC = 8
L_FULL = 4
SEQ = 4096
NMETA = 16
NB = 33
TP = NB * 128
NBT = 3
NT = NBT * 128
NTILES_FULL = NB // NBT
INC = 6408
DFF = 2816
FK = DFF // 128
QA, KA, VA, QF, KF, VF, FL, XC, YC, GT = 0, 512, 640, 768, 1280, 1792, 2304, 2312, 2824, 3336
SLOT = 4608
NRING = 3
ND = 40
NDP = 16
NEG = -30000.0
EPS = 1e-6
VW = 768
ENGS = ("pe", "act", "dve", "pool", "sp")


ALLBUFS = []


class Buf:
    __slots__ = ("w", "r", "name")

    def __init__(self, name=""):
        self.w = None
        self.r = []
        self.name = name
        ALLBUFS.append(self)


class Op:
    __slots__ = ("eng", "fn", "deps", "signal", "sigval", "is_dma", "didx")


class Prog:
    def __init__(self):
        self.ops = {e: [] for e in ENGS}
        self.ndma = 0
        self.ndma_p = 0

    def add(self, eng, fn, reads=(), writes=(), dma=False):
        op = Op()
        op.eng, op.fn, op.signal, op.sigval, op.is_dma, op.didx = eng, fn, False, 0, dma, -1
        deps = set()
        for b in reads:
            if b.w is not None:
                deps.add(b.w)
        for b in writes:
            if b.w is not None:
                deps.add(b.w)
            deps.update(b.r)
        for b in reads:
            if not dma:
                b.r = [o for o in b.r if o.is_dma or o.eng != eng]
            b.r.append(op)
        for b in writes:
            b.w = op
            b.r = []
        op.deps = [d for d in deps if d.is_dma or not (d.eng == "pe" and eng == "pe")]
        for d in op.deps:
            d.signal = True
        if dma:
            if eng == "pool":
                op.didx = self.ndma_p
                self.ndma_p += 1
            else:
                op.didx = self.ndma
                self.ndma += 1
        self.ops[eng].append(op)
        return op

    def barrier(self):
        lasts = [self.ops[e][-1] for e in ENGS if self.ops[e] and not self.ops[e][-1].is_dma]
        for e in ("pe", "act", "dve", "pool"):
            for o in reversed(self.ops[e]):
                if o.fn is not None:
                    lasts.append(o)
                    break
        dmas = [o for e_ in ENGS for o in self.ops[e_] if o.is_dma]
        for e in ENGS:
            op = self.add(e, None)
            op.deps = list(set(lasts)) + dmas
            for d in op.deps:
                d.signal = True
        for b in ALLBUFS:
            b.w = None
            b.r = []

    def emit(self, nc, sems, dsems, base):
        for e in ENGS:
            cnt = base.get(("e", e), 0)
            for op in self.ops[e]:
                if op.is_dma:
                    continue
                if op.signal:
                    cnt += 1
                op.sigval = cnt
            base[("e", e)] = cnt
        d0 = base.get("dma", 0)

        def dkey(d):
            if d.eng == "pool":
                return ("q", d.didx % NDP), 16 * (d.didx // NDP + 1)
            gi = d.didx + d0
            return ("d", gi % ND), 16 * (gi // ND + 1)

        def run(e, eng):
            known = {}
            for op in self.ops[e]:
                waits = {}
                for d in op.deps:
                    if d.is_dma:
                        key, val = dkey(d)
                    else:
                        key, val = ("e", d.eng), d.sigval
                    if waits.get(key, 0) < val:
                        waits[key] = val
                if op.is_dma:
                    key, val = dkey(op)
                    val -= 16
                    if val > 0 and waits.get(key, 0) < val:
                        waits[key] = val
                for key, val in waits.items():
                    if known.get(key, 0) >= val:
                        continue
                    s = dsems[key[1]] if key[0] == "d" else (dsems[ND + key[1]] if key[0] == "q" else sems[key[1]])
                    eng.wait_ge(s, val)
                    known[key] = val
                if op.fn is None:
                    continue
                ins = op.fn(eng)
                if op.is_dma:
                    k_ = dkey(op)[0]
                    ins.then_inc(dsems[k_[1]] if k_[0] == "d" else dsems[ND + k_[1]], 16)
                elif op.signal:
                    ins.then_inc(sems[e], 1)

        with nc.Block() as block:
            @block.tensor
            def _(eng):
                run("pe", eng)

            @block.scalar
            def _(eng):
                run("act", eng)

            @block.vector
            def _(eng):
                run("dve", eng)

            @block.gpsimd
            def _(eng):
                run("pool", eng)

            @block.sync
            def _(eng):
                run("sp", eng)
        base["dma"] = d0 + self.ndma


def slab_defs():
    sl = []
    sl.append(("xc", 4096, [("w_in", 0, 8, XC, 512, 0, 512)]))
    sl.append(("yc", 4096, [("w_in", 0, 8, YC, 512, 0, 512)]))
    sl.append(("qf", 4096, [("w_in", 0, 8, QF, 512, 0, 512)]))
    sl.append(("kf", 4096, [("w_in", 0, 8, KF, 512, 0, 512)]))
    sl.append(("vf", 4096, [("w_in", 0, 8, VF, 512, 0, 512)]))
    sl.append(("kav", 8 * 264, [("w_in", 0, 8, KA, 128, 0, 264), ("w_in", 0, 8, VA, 128, 128, 264),
                                ("w_in", 0, 8, FL, 8, 256, 264)]))
    pc = []
    for c in range(4):
        pc.append(("w_in", 0, 8, QA + c * 64, 64, c * 128, 512))
        pc.append(("w_in", 0, 8, QA + (c + 4) * 64, 64, c * 128 + 64, 512))
    sl.append(("qa", 4096, pc))
    for m in range(8):
        pc = []
        for x in range(3):
            pc.append(("w_in", 0, 8, GT + x * 1024 + m * 128, 128, x * 128, 384))
        for x in range(3):
            pc.append(("w_branch%d" % x, 0, 4, m * 128, 128, 3072 + x * 128, 384))
        sl.append(("mg%d" % m, 3072 + 1536, pc))
    for hf in range(2):
        sl.append(("wo%d" % hf, 4096, [("w_out", 0, 8, hf * 512, 512, 0, 512)]))
    for s in range(11):
        pc = []
        for pp in range(2):
            j = 2 * s + pp
            pc.append(("w_ffn_in", 0, 8, j * 128, 128, (2 * pp) * 128, 512))
            pc.append(("w_ffn_in", 0, 8, DFF + j * 128, 128, (2 * pp + 1) * 128, 512))
        sl.append(("fi%d" % s, 4096, pc))
    for m in range(8):
        sl.append(("fo%d" % m, FK * 128, [("w_ffn_out", 0, FK, m * 128, 128, 0, 128)]))
    return sl


SLABS = slab_defs()
NSLAB = len(SLABS)
SLAB_IDX = {s[0]: i for i, s in enumerate(SLABS)}
PR_NMIX, PR_NFFN, PR_NFIN, PR_CW, PR_CB, PR_BR, PR_BI, PR_LAM = 0, 32, 64, 72, 136, 152, 168, 184
NPRM = 200


def build(n_layers=L_FULL, n_tiles=NTILES_FULL, dbg=False):
    nc = bass.Bass("TRN2", target_bir_lowering=False)
    dt_in = lambda n, s: nc.dram_tensor(n, s, F32, kind="ExternalInput").ap()
    x = dt_in("x", [SEQ, D])
    meta = dt_in("meta_tokens", [NMETA, D])
    relb = dt_in("rel_bias_table", [32, 8])
    norm_mix = dt_in("norm_mix", [L_FULL, D])
    w_in = dt_in("w_in", [L_FULL, D, INC])
    sinks = dt_in("swa_sinks", [L_FULL, 8])
    fbias = dt_in("fox_forget_bias", [L_FULL, 8])
    conv_w = dt_in("conv_w", [L_FULL, 4, 512])
    conv_b = dt_in("conv_b", [L_FULL, 512])
    lru_w_r = dt_in("lru_w_r", [L_FULL, 8, 64, 64])
    lru_b_r = dt_in("lru_b_r", [L_FULL, 512])
    lru_w_i = dt_in("lru_w_i", [L_FULL, 8, 64, 64])
    lru_b_i = dt_in("lru_b_i", [L_FULL, 512])
    lru_lam = dt_in("lru_lambda", [L_FULL, 512])
    w_branch = dt_in("w_branch", [L_FULL, 3, 512, D])
    w_out = dt_in("w_out", [L_FULL, D, D])
    norm_ffn = dt_in("norm_ffn", [L_FULL, D])
    w_ffn_in = dt_in("w_ffn_in", [L_FULL, D, 2 * DFF])
    w_ffn_out = dt_in("w_ffn_out", [L_FULL, DFF, D])
    norm_fin = dt_in("norm_final", [D])
    cst = dt_in("consts", [128, 384])
    out = nc.dram_tensor("out", [SEQ, D], F32, kind="ExternalOutput").ap()
    if dbg:
        dbgo = nc.dram_tensor("dbgo", [128, 12, NT], BF16, kind="ExternalOutput").ap()
    wbf = nc.dram_tensor("wbf", [n_layers * NSLAB, 128, SLOT], BF16, kind="Internal").ap()
    lrubf = nc.dram_tensor("lrubf", [n_layers, 128, 1024], BF16, kind="Internal").ap()
    hbuf = nc.dram_tensor("hbuf", [128, KC, TP], F32, kind="Internal").ap()
    extd = nc.dram_tensor("extd", [8, 384], F32, kind="Internal").ap()

    srcs = {"w_in": w_in, "w_out": w_out, "w_ffn_in": w_ffn_in, "w_ffn_out": w_ffn_out}

    def wsrc(name, l):
        if name.startswith("w_branch"):
            return w_branch[l, int(name[-1])]
        return srcs[name][l]

    sb = lambda n, s, d: nc.sbuf_tensor(n, s, d)
    import contextlib
    if True:
        with contextlib.ExitStack() as stk:
            identf = stk.enter_context(sb("identf", [128, 128], F32))
            trif = stk.enter_context(sb("trif", [128, 128], F32))
            onesf = stk.enter_context(sb("onesf", [128, 128], F32))
            trib = stk.enter_context(sb("trib", [128, 128], BF16))
            prm = stk.enter_context(sb("prm", [128, NPRM], F32))
            ls8 = stk.enter_context(sb("ls8", [128, 16], F32))
            ls16 = stk.enter_context(sb("ls16", [128, 16], F32))
            es = stk.enter_context(sb("es", [128, 32], F32))
            fb = stk.enter_context(sb("fb", [128, 32], F32))
            epsb = stk.enter_context(sb("epsb", [128, 1], F32))
            onesb = stk.enter_context(sb("onesb", [128, 128], BF16))
            swab = stk.enter_context(sb("swab", [128, 4, 512], F32))
            psum = stk.enter_context(nc.psum_tensor("psum", [128, 8, 512], F32))
            s_pe = stk.enter_context(nc.semaphore("s_pe"))
            s_act = stk.enter_context(nc.semaphore("s_act"))
            s_dve = stk.enter_context(nc.semaphore("s_dve"))
            s_pool = stk.enter_context(nc.semaphore("s_pool"))
            s_sp = stk.enter_context(nc.semaphore("s_sp"))
            kTf = stk.enter_context(sb("kTf", [128, 4, TP], BF16))
            vfx = stk.enter_context(sb("vf", [128, NB, VW], BF16))
            negcum = stk.enter_context(sb("negcum", [128, NB, 8], F32))
            btab = stk.enter_context(sb("btab", [128, NB, 8], F32))
            hT = stk.enter_context(sb("hT", [128, 2, KC, NT], F32))
            uT = stk.enter_context(sb("uT", [128, KC, NT], BF16))
            wr = stk.enter_context(sb("wr", [128, NRING, SLOT], BF16))
            R = stk.enter_context(sb("R", [128, 24, NT], BF16))
            W = stk.enter_context(sb("W", [128, 6, NT], F32))
            xcT = stk.enter_context(sb("xcT", [128, 4, NT + 3], F32))
            rstd = stk.enter_context(sb("rstd", [128, NT], F32))
            pT = stk.enter_context(sb("pT", [128, 6, 512], BF16))
            kTa = stk.enter_context(sb("kTa", [128, 128 + NT], BF16))
            vsw = stk.enter_context(sb("vsw", [128, NBT + 1, 320], BF16))
            swt = stk.enter_context(sb("swt", [128, 1, 512], F32))
            rd = stk.enter_context(sb("rd", [128, 2, 512], F32))
            lruw = stk.enter_context(sb("lruw", [128, 1024], BF16))
            cvb = stk.enter_context(sb("cvb", [128, 2, NT], BF16))
            sm = stk.enter_context(sb("sm", [128, 4, 8], F32))
            rrun = stk.enter_context(sb("rrun", [128, 8], F32))
            nref = stk.enter_context(sb("nref", [128, 8], F32))
            hstate = stk.enter_context(sb("hstate", [128, 4], F32))
            dsems = [stk.enter_context(nc.semaphore("s_d%d" % i)) for i in range(ND + NDP)]
            sems = {"pe": s_pe, "act": s_act, "dve": s_dve, "pool": s_pool, "sp": s_sp}
            base = {}
            B_const = Buf("const")
            B_swab = Buf("swab")
            B_prm = Buf("prm")
            B_hb = [Buf("hb%d" % i) for i in range(NTILES_FULL)]
            B_wbf = [Buf("wbf%d" % i) for i in range(n_layers * NSLAB)]
            B_lrubf = Buf("lrubf")
            PB = [Buf("ps%d" % i) for i in range(8)]

            P = Prog()
            if True:
                stg = vfx[:].rearrange("p a b -> p (a b)").bitcast(F32)[:, 0:2 * SLOT].rearrange("p (a b) -> p a b", a=2)
                stb = kTf[:].rearrange("p a b -> p (a b)")[:, 0:2 * SLOT].rearrange("p (a b) -> p a b", a=2)
                xin = hT[:].rearrange("p a b c -> p (a b c)")[:, 0:2 * D].rearrange("p (a b) -> p a b", a=2)
                hst = W[:].rearrange("p a b -> p (a b)")[:, 0:2 * KC * 128].rearrange("p (a c t) -> p a c t", a=2, c=KC)
                prow = rstd[:, 0:256].rearrange("p (a b) -> p a b", a=2)
                tbl = rstd[0:32, 256:264]
                oh = swt[0:32, 0, 0:128]
                ext = swt[0:8, 0, 128:512]
                lst = uT[:].rearrange("p a b -> p (a b)").bitcast(F32)[:, 0:1024]
                lsb = R[:].rearrange("p a b -> p (a b)")[:, 0:1024]
                tmp16 = sm[:].rearrange("p a b -> p (a b)")[:, 0:16]
                swraw = wr[:].rearrange("p a b -> p (a b)").bitcast(F32)[:, 0:2048].rearrange("p (h q) -> p h q", h=8)
                stg_flat = vfx[:].rearrange("p a b -> p (a b)").bitcast(F32)
                B_stg = [Buf(), Buf()]
                B_stb = [Buf(), Buf()]
                B_xin = [Buf(), Buf()]
                B_hst = [Buf(), Buf()]
                B_prow = Buf()
                B_misc = Buf()
                B_ext = Buf()
                B_extd = Buf()
                B_lst = Buf()
                B_lsb = Buf()
                P.add("sp", lambda e: e.dma_start(out=identf[:], in_=cst[:, 0:128]), writes=[B_const], dma=True)
                P.add("sp", lambda e: e.dma_start(out=trif[:], in_=cst[:, 128:256]), writes=[B_const], dma=True)
                P.add("sp", lambda e: e.dma_start(out=oh[:], in_=cst[0:32, 256:384]), writes=[B_misc], dma=True)
                P.add("sp", lambda e: e.dma_start(out=tbl[:], in_=relb), writes=[B_misc], dma=True)
                P.add("pool", lambda e: e.memset(onesf[:], 1.0), writes=[B_const])
                P.add("pool", lambda e: e.memset(epsb[:], EPS), writes=[B_const])
                P.add("pool", lambda e: e.memset(onesb[:], 1.0), writes=[B_const])
                P.add("dve", lambda e: e.tensor_copy(out=trib[:], in_=trif[:]), reads=[B_const], writes=[B_const])
                rows = []
                rows += [(PR_NMIX, norm_mix.rearrange("l (c p) -> (l c) p", p=128), 32)]
                rows += [(PR_NFFN, norm_ffn.rearrange("l (c p) -> (l c) p", p=128), 32)]
                rows += [(PR_NFIN, norm_fin.rearrange("(c p) -> c p", p=128), 8)]
                rows += [(PR_CW, conv_w.rearrange("l t (c p) -> (l t c) p", p=128), 64)]
                rows += [(PR_CB, conv_b.rearrange("l (c p) -> (l c) p", p=128), 16)]
                rows += [(PR_BR, lru_b_r.rearrange("l (c p) -> (l c) p", p=128), 16)]
                rows += [(PR_BI, lru_b_i.rearrange("l (c p) -> (l c) p", p=128), 16)]
                rows += [(PR_LAM, lru_lam.rearrange("l (c p) -> (l c) p", p=128), 16)]
                B_prow_l = [Buf() for _ in range(24)]
                prow_i = [0]
                P.add("pool", lambda e: e.memset(prow[:], 0.0), writes=B_prow_l)
                for (r0, src, n) in rows:
                    a = r0
                    while a < r0 + n:
                        g = a // 128
                        b_ = min(r0 + n, (g + 1) * 128)
                        P.add("sp", (lambda e, g=g, a=a, b_=b_, src=src, r0=r0: e.dma_start(
                            out=prow[a - g * 128:b_ - g * 128, g, :], in_=src[a - r0:b_ - r0, :])),
                            writes=[B_prow_l[prow_i[0]]], dma=True)
                        prow_i[0] += 1
                        a = b_
                for g in range(2):
                    ncol = min(128, NPRM - g * 128)
                    P.add("pe", (lambda e, g=g: e.transpose(psum[:, g, 0:128], prow[:, g, :], identf[:])),
                          reads=B_prow_l + [B_const], writes=[PB[g]])
                    P.add("dve", (lambda e, g=g, ncol=ncol: e.tensor_copy(out=prm[:, g * 128:g * 128 + ncol],
                                                                      in_=psum[:, g, 0:ncol])),
                          reads=[PB[g]], writes=[B_prm])
                P.add("act", lambda e: e.activation(out=tmp16[:], in_=prm[:, PR_LAM:PR_LAM + 16], func=AF.Exp, scale=-1.0),
                      reads=[B_prm], writes=[B_misc])
                P.add("act", lambda e: e.activation(out=tmp16[:], in_=tmp16[:], func=AF.Ln, bias=1.0),
                      reads=[B_misc], writes=[B_misc])
                P.add("dve", lambda e: e.tensor_scalar(out=ls8[:], in0=tmp16[:], scalar1=-8.0, scalar2=None, op0=ALU.mult),
                      reads=[B_misc], writes=[B_prm])
                P.add("dve", lambda e: e.tensor_scalar(out=ls16[:], in0=tmp16[:], scalar1=-16.0, scalar2=None, op0=ALU.mult),
                      reads=[B_misc], writes=[B_prm])
                P.add("sp", lambda e: e.dma_start(out=es[:], in_=sinks.rearrange("l h -> (l h)").partition_broadcast(128)),
                      writes=[B_misc], dma=True)
                P.add("sp", lambda e: e.dma_start(out=fb[:], in_=fbias.rearrange("l h -> (l h)").partition_broadcast(128)),
                      writes=[B_prm], dma=True)
                P.add("act", lambda e: e.activation(out=es[:], in_=es[:], func=AF.Exp), reads=[B_misc], writes=[B_prm])
                P.add("pe", lambda e: e.matmul(psum[0:8, 2, 0:128], lhsT=tbl[:, :], rhs=oh[:, :], start=True, stop=True),
                      reads=[B_misc], writes=[PB[2]])
                P.add("pool", lambda e: e.memset(ext[:], NEG), writes=[B_ext])
                P.add("dve", lambda e: e.tensor_copy(out=ext[:, 128:256], in_=psum[0:8, 2, 0:128]),
                      reads=[PB[2], B_ext], writes=[B_ext])
                P.add("sp", lambda e: e.dma_start(out=extd, in_=ext[:]), reads=[B_ext], writes=[B_extd], dma=True)
                SLOT2HH = [0, 2, 1, 3]
                B_swraw_l = [Buf() for _ in range(128)]
                for ki in range(128):
                    P.add("sp", (lambda e, ki=ki: e.dma_start(out=swraw[ki:ki + 1, :, :],
                                                              in_=extd[:, 128 - ki:384 - ki])),
                          reads=[B_extd], writes=[B_swraw_l[ki]], dma=True)
                for g in range(2):
                    for kind in range(2):
                        for slot in range(4):
                            h = 4 * g + SLOT2HH[slot]
                            so = 128 if kind == 0 else 0
                            P.add("dve", (lambda e, g=g, kind=kind, slot=slot, h=h, so=so: e.tensor_copy(
                                out=swab[:, g * 2 + kind, slot * 128:(slot + 1) * 128], in_=swraw[:, h, so:so + 128])),
                                reads=B_swraw_l, writes=[B_swab])
                for l in range(n_layers):
                    B_lst_l = [Buf() for _ in range(16)]
                    P.add("pool", lambda e: e.memset(lst[:], 0.0), reads=[B_lst], writes=B_lst_l)
                    for ri, wsrc_ in enumerate((lru_w_r, lru_w_i)):
                        for j in range(4):
                            for hb_ in range(2):
                                blk = 2 * j + hb_
                                P.add("sp", (lambda e, l=l, ri=ri, j=j, hb_=hb_, blk=blk, wsrc_=wsrc_: e.dma_start(
                                    out=lst[hb_ * 64:(hb_ + 1) * 64, (ri * 4 + j) * 128 + hb_ * 64:(ri * 4 + j) * 128 + hb_ * 64 + 64],
                                    in_=wsrc_[l, blk])), writes=[B_lst_l[ri * 8 + j * 2 + hb_]], dma=True)
                    P.add("dve", lambda e: e.tensor_copy(out=lsb[:], in_=lst[:]), reads=B_lst_l, writes=[B_lsb, B_lst])
                    P.add("sp", (lambda e, l=l: e.dma_start(out=lrubf[l], in_=lsb[:])), reads=[B_lsb],
                          writes=[B_lrubf], dma=True)
                def conv_dmas(l):
                    out_ = []
                    for si, (nm, nel, pieces) in enumerate(SLABS):
                        idx = l * NSLAB + si
                        for pi_, (src, row0, nk, col0, ncols, doff, dks) in enumerate(pieces):
                            Wsrc = wsrc(src, l)
                            import os
                            if ncols < int(os.environ.get('MINCOLS', '0')):
                                continue
                            for k0 in range(0, nk, 8):
                                k1 = min(nk, k0 + 8)
                                sap = Wsrc[row0 + k0 * 128:row0 + k1 * 128, col0:col0 + ncols].rearrange("(k p) c -> p k c", p=128)
                                dap = bass.AP(wbf.tensor, idx * 128 * SLOT + doff + k0 * dks, [[SLOT, 128], [dks, k1 - k0], [1, ncols]])
                                out_.append((idx, sap, dap))
                    return out_

                def rec_conv(item):
                    idx, sap, dap = item
                    op = P.add("pool", (lambda e, sap=sap, dap=dap: e.dma_start(out=dap, in_=sap)), dma=True)
                    CONVW.setdefault(idx, []).append(op)

                CONVW = {}
                P.barrier()
                conv0 = {}
                for item in conv_dmas(0):
                    conv0.setdefault(item[0], []).append(item)

            if True:
                B_kTf = [[Buf() for _ in range(NTILES_FULL)] for _ in range(4)]
                B_vf = [Buf() for _ in range(NB)]
                B_vones = Buf()
                B_negcum = Buf()
                B_btab = Buf()
                B_hT = [[Buf() for _ in range(KC)] for _ in range(2)]
                B_uT = [Buf() for _ in range(KC)]
                B_wr = [Buf() for _ in range(NRING)]
                B_R = [Buf() for _ in range(24)]
                B_W = [Buf() for _ in range(6)]
                B_xc = [Buf() for _ in range(4)]
                B_rstd = Buf()
                B_pT = [Buf() for _ in range(6)]
                B_kTa = Buf()
                B_vsw = [Buf() for _ in range(NBT + 1)]
                B_swt = [Buf(), Buf()]
                B_rd = [Buf(), Buf()]
                B_lruw = Buf()
                B_cvb = [Buf(), Buf()]
                B_sm = [Buf() for _ in range(4)]
                B_rrun = Buf()
                B_nref = Buf()
                B_hst8 = Buf()
                sq = cvb
                B_sq = B_cvb
                Wflat = W[:].rearrange('p a b -> p (a b)')
                ost = [Wflat[:, 0:D], Wflat[:, 3 * NT:3 * NT + D]]
                B_ost = [[B_W[0], B_W[1], B_W[2]], [B_W[3], B_W[4], B_W[5]]]
                st = {"cv": 0, "psA": 0, "psF": 0, "psG": 0, "acc": 0, "pt": 0, "pts": 0, "ld": 0}

                RINGS = {"A": [0, 1, 2, 3, 4, 5], "F": [0, 1, 2, 3], "G": [4, 5]}

                def ps_next(ring="A"):
                    r = RINGS[ring]
                    i = r[st["ps" + ring] % len(r)]
                    st["ps" + ring] += 1
                    return i

                def acc_next():
                    i = 6 + st["acc"] % 2
                    st["acc"] += 1
                    return i

                def pt_next():
                    i = st["pt"] % 4
                    st["pt"] += 1
                    return i

                def MM(o, lt, rh, start, stop, reads, writes):
                    return P.add("pe", (lambda e: e.matmul(o, lhsT=lt, rhs=rh, start=start, stop=stop)), reads=reads, writes=writes)

                def ACT(o, i, func, reads, writes, bias=None, scale=None):
                    kw = {}
                    if bias is not None:
                        kw["bias"] = bias
                    if scale is not None:
                        kw["scale"] = scale
                    return P.add("act", (lambda e: e.activation(out=o, in_=i, func=func, **kw)), reads=reads, writes=writes)

                def TT(eng, o, a, b, op, reads, writes):
                    return P.add(eng, (lambda e: e.tensor_tensor(out=o, in0=a, in1=b, op=op)), reads=reads, writes=writes)

                def TS(eng, o, a, s1, s2, op0, op1, reads, writes):
                    if op1 is None:
                        return P.add(eng, (lambda e: e.tensor_scalar(out=o, in0=a, scalar1=s1, scalar2=None, op0=op0)), reads=reads, writes=writes)
                    return P.add(eng, (lambda e: e.tensor_scalar(out=o, in0=a, scalar1=s1, scalar2=s2, op0=op0, op1=op1)), reads=reads, writes=writes)

                def STT(eng, o, a, s, b, op0, op1, reads, writes):
                    eng = "dve"
                    return P.add(eng, (lambda e: e.scalar_tensor_tensor(out=o, in0=a, scalar=s, in1=b, op0=op0, op1=op1)), reads=reads, writes=writes)

                def RECIP(o, i, reads, writes):
                    return P.add("dve", (lambda e: e.reciprocal(out=o, in_=i)), reads=reads, writes=writes)

                def CP(eng, o, i, reads, writes):
                    if eng == "act":
                        return ACT(o, i, AF.Copy, reads, writes)
                    return P.add(eng, (lambda e: e.tensor_copy(out=o, in_=i)), reads=reads, writes=writes)

                def DMA(o, i, reads, writes):
                    return P.add("sp", (lambda e: e.dma_start(out=o, in_=i)), reads=reads, writes=writes, dma=True)

                total_slabs = n_layers * n_tiles * NSLAB

                def slab_global(l, ti, si):
                    return (l * n_tiles + ti) * NSLAB + si

                def ensure_loaded(upto):
                    upto = min(upto, total_slabs - 1)
                    while st["ld"] <= upto:
                        gidx = st["ld"]
                        if gidx < NSLAB:
                            for ci in range(st["cv"], min(NSLAB, gidx + 7)):
                                for item in conv0.get(ci, []):
                                    rec_conv(item)
                            st["cv"] = max(st["cv"], min(NSLAB, gidx + 7))
                        l = gidx // (n_tiles * NSLAB)
                        si = gidx % NSLAB
                        nel = SLABS[si][1]
                        slot = gidx % NRING
                        op_ = DMA(wr[:, slot, 0:nel], wbf[l * NSLAB + si, :, 0:nel], [B_wbf[l * NSLAB + si]], [B_wr[slot]])
                        for cop in CONVW.get(l * NSLAB + si, []):
                            if cop not in op_.deps:
                                op_.deps.append(cop)
                        st["ld"] += 1

                def use_slab(l, ti, name):
                    g = slab_global(l, ti, SLAB_IDX[name])
                    ensure_loaded(g + NRING - 1)
                    return g % NRING

                def wv(slot, off, kstride, k, c0, ncols):
                    return wr[:, slot, off + k * kstride + c0: off + k * kstride + c0 + ncols]

                P.add("pool", lambda e: e.memset(vfx[:].rearrange("p b (c w) -> p (b c) w", w=192)[:, :, 64:128], 1.0), writes=[B_vones])
                for s_ in range(NBT + 1):
                    P.add("pool", (lambda e, s_=s_: e.memset(vsw[:, s_, :], 1.0)), writes=[B_vsw[s_]])

                def rmsnorm_to_uT(hb, gcol):
                    ssb = ps_next()
                    for c in range(KC):
                        q = c % 2
                        ACT(sq[:, q, :], hT[:, hb, c, :], AF.Square, [B_hT[hb][c]], [B_sq[q]])
                        MM(psum[:, ssb, 0:NT], onesb[:, :], sq[:, q, :], c == 0, c == KC - 1, [B_const, B_sq[q]], [PB[ssb]])
                    ACT(rstd[:, :], psum[:, ssb, 0:NT], AF.Ln, [PB[ssb]], [B_rstd], bias=epsb[:, 0:1], scale=1.0 / D)
                    ACT(rstd[:, :], rstd[:, :], AF.Exp, [B_rstd], [B_rstd], scale=-0.5)
                    for c in range(KC):
                        eng = "dve" if c % 2 == 0 else "pool"
                        STT(eng, uT[:, c, :], hT[:, hb, c, :], prm[:, gcol + c:gcol + c + 1], rstd[:, :], ALU.mult, ALU.mult,
                            [B_hT[hb][c], B_prm, B_rstd], [B_uT[c]])

                rdflat = rd[:].rearrange("p a b -> p (a b)")

                def load_x_tile(tt, hbt):
                    for bb in range(NBT):
                        b = tt * NBT + bb
                        if b == 0:
                            DMA(rdflat[0:16, :], meta, [], B_rd)
                            DMA(rdflat[16:128, :], x[0:112, :], [], B_rd)
                        elif b == NB - 1:
                            P.add("pool", lambda e: e.memset(rdflat[:, :], 0.0), writes=B_rd)
                            DMA(rdflat[0:16, :], x[SEQ - 16:SEQ, :], [], B_rd)
                        else:
                            DMA(rdflat[:, :], x[128 * b - 16:128 * b + 112, :], [], B_rd)
                        for half in range(2):
                            pb = ps_next()
                            for cc in range(4):
                                c = half * 4 + cc
                                P.add("pe", (lambda e, pb=pb, cc=cc, c=c: e.transpose(psum[:, pb, cc * 128:(cc + 1) * 128], rdflat[:, c * 128:(c + 1) * 128], identf[:])),
                                      reads=B_rd + [B_const], writes=[PB[pb]])
                            CP("act" if half else "dve", hT[:, hbt, half * 4:half * 4 + 4, bb * 128:(bb + 1) * 128],
                               psum[:, pb, :].rearrange("p (c t) -> p c t", c=4), [PB[pb]], B_hT[hbt][half * 4:half * 4 + 4])

                for l in range(n_layers):
                    last_layer = (l == n_layers - 1)
                    DMA(lruw[:, :], lrubf[l], [B_lrubf], [B_lruw])
                    P.add("pool", lambda e: e.memset(rrun[:], 0.0), writes=[B_rrun])
                    P.add("pool", lambda e: e.memset(hstate[:], 0.0), writes=[B_hst8])
                    for j in range(4):
                        P.add("pool", (lambda e, j=j: e.memset(xcT[:, j, 0:3], 0.0)), writes=[B_xc[j]])
                    if l == 0:
                        load_x_tile(0, 0)
                    else:
                        DMA(hT[:, 0, :, :], hbuf[:, :, 0:NT], [B_hb[0]], B_hT[0])
                    for ti in range(n_tiles):
                        hb = ti % 2
                        b0 = ti * NBT
                        t0 = ti * NT
                        if ti + 1 < n_tiles:
                            if l == 0:
                                load_x_tile(ti + 1, 1 - hb)
                            else:
                                DMA(hT[:, 1 - hb, :, :], hbuf[:, :, t0 + NT:t0 + 2 * NT], [B_hb[ti + 1]], B_hT[1 - hb])
                        if l + 1 < n_layers:
                            if ti == 0:
                                conv_next = conv_dmas(l + 1)
                            per = -(-len(conv_next) // n_tiles)
                            for item in conv_next[ti * per:(ti + 1) * per]:
                                rec_conv(item)
                        rmsnorm_to_uT(hb, PR_NMIX + l * 8)
                        nkb = b0 + NBT

                        def proj_fm(slot, off, kstride, nch, evac, ring="A"):
                            for j in range(nch):
                                pb = ps_next(ring)
                                for k in range(KC):
                                    MM(psum[:, pb, 0:NT], wv(slot, off, kstride, k, j * 128, 128), uT[:, k, :], k == 0, k == KC - 1,
                                       [B_wr[slot], B_uT[k]], [PB[pb]])
                                evac(j, pb)
                                yield

                        def run(gen):
                            for _ in gen:
                                pass

                        s5 = use_slab(l, ti, "xc")
                        run(proj_fm(s5, 0, 512, 4, lambda j, pb: CP("act", xcT[:, j, 3:3 + NT], psum[:, pb, 0:NT], [PB[pb]], [B_xc[j]])))
                        s6 = use_slab(l, ti, "yc")

                        def gelu_evac(j, pb):
                            ACT(R[:, 20 + j, :], psum[:, pb, 0:NT], AF.Gelu_apprx_tanh, [PB[pb]], [B_R[20 + j]])

                        run(proj_fm(s6, 0, 512, 4, gelu_evac))
                        def parta_gen():
                            s2 = use_slab(l, ti, "qf")
                            yield from proj_fm(s2, 0, 512, 4, lambda j, pb: ACT(R[:, 4 + j, :], psum[:, pb, 0:NT], AF.Identity, [PB[pb]], [B_R[4 + j]], scale=0.125), ring="F")
                            s3 = use_slab(l, ti, "kf")
                            yield from proj_fm(s3, 0, 512, 4, lambda j, pb: CP("dve", kTf[:, j, t0:t0 + NT], psum[:, pb, 0:NT], [PB[pb]], [B_kTf[j][ti]]), ring="F")
                            s4 = use_slab(l, ti, "vf")
                            for bb in range(NBT):
                                pb = ps_next("F")
                                for k in range(KC):
                                    MM(psum[:, pb, 0:512], uT[:, k, bb * 128:(bb + 1) * 128], wv(s4, 0, 512, k, 0, 512), k == 0, k == KC - 1,
                                       [B_wr[s4], B_uT[k]], [PB[pb]])
                                P.add("dve" if bb % 2 == 0 else "act", (lambda e, pb=pb, bb=bb, b0=b0, isact=(bb % 2 == 1): (
                                    e.activation(out=bass.AP(vfx, (b0 + bb) * VW, [[NB * VW, 128], [192, 4], [128, 2], [1, 64]]),
                                                 in_=psum[:, pb, 0:512].rearrange("p (c g d) -> p c g d", c=4, g=2), func=AF.Copy) if isact else
                                    e.tensor_copy(out=bass.AP(vfx, (b0 + bb) * VW, [[NB * VW, 128], [192, 4], [128, 2], [1, 64]]),
                                                  in_=psum[:, pb, 0:512].rearrange("p (c g d) -> p c g d", c=4, g=2)))),
                                    reads=[PB[pb]], writes=[B_vf[b0 + bb]])
                                yield
                            s1 = use_slab(l, ti, "kav")
                            if ti > 0:
                                CP("pool", kTa[:, 0:128], kTa[:, NT:NT + 128], [B_kTa], [B_kTa])
                                CP("pool", vsw[:, 0, :], vsw[:, NBT, :], [B_vsw[NBT]], [B_vsw[0]])
                            yield from proj_fm(s1, 0, 264, 1, lambda j, pb: CP("dve", kTa[:, 128:128 + NT], psum[:, pb, 0:NT], [PB[pb]], [B_kTa]), ring="F")
                            for bb in range(NBT):
                                pb = ps_next("F")
                                for k in range(KC):
                                    MM(psum[:, pb, 0:128], uT[:, k, bb * 128:(bb + 1) * 128], wv(s1, 0, 264, k, 128, 128), k == 0, k == KC - 1,
                                       [B_wr[s1], B_uT[k]], [PB[pb]])
                                P.add("dve", (lambda e, pb=pb, bb=bb: e.tensor_copy(
                                    out=bass.AP(vsw, (bb + 1) * 320 + 64, [[(NBT + 1) * 320, 128], [128, 2], [1, 64]]),
                                    in_=psum[:, pb, 0:128].rearrange("p (g d) -> p g d", g=2))),
                                    reads=[PB[pb]], writes=[B_vsw[bb + 1]])
                                pb2 = ps_next("F")
                                for k in range(KC):
                                    MM(psum[:, pb2, 0:8], uT[:, k, bb * 128:(bb + 1) * 128], wv(s1, 0, 264, k, 256, 8), k == 0, k == KC - 1,
                                       [B_wr[s1], B_uT[k]], [PB[pb2]])
                                q = bb % 4
                                TT("dve", sm[:, q, :], psum[:, pb2, 0:8], fb[:, l * 8:(l + 1) * 8], ALU.add, [PB[pb2], B_prm], [B_sm[q]])
                                ACT(sm[:, q, :], sm[:, q, :], AF.Exp, [B_sm[q]], [B_sm[q]], scale=-1.0)
                                ACT(sm[:, q, :], sm[:, q, :], AF.Ln, [B_sm[q]], [B_sm[q]], bias=1.0)
                                pc_ = ps_next("F")
                                MM(psum[:, pc_, 0:8], trif[:, :], sm[:, q, :], True, True, [B_const, B_sm[q]], [PB[pc_]])
                                MM(psum[:, pc_, 8:16], onesf[:, :], sm[:, q, :], True, True, [B_const, B_sm[q]], [PB[pc_]])
                                if bb == 0:
                                    CP("dve", nref[:, :], rrun[:, :], [B_rrun], [B_nref])
                                TT("dve", negcum[:, b0 + bb, :], psum[:, pc_, 0:8], rrun[:, :], ALU.add, [PB[pc_], B_rrun], [B_negcum])
                                TT("dve", rrun[:, :], psum[:, pc_, 8:16], rrun[:, :], ALU.add, [PB[pc_], B_rrun], [B_rrun])
                                yield
                            TT("dve", nref[:, :], nref[:, :], rrun[:, :], ALU.add, [B_nref, B_rrun], [B_nref])
                            TS("dve", nref[:, :], nref[:, :], 0.5, None, ALU.mult, None, [B_nref], [B_nref])
                            P.add("dve", (lambda e, nkb=nkb: e.tensor_tensor(
                                out=btab[:, 0:nkb, :], in0=negcum[:, 0:nkb, :],
                                in1=bass.AP(nref, 0, [[8, 128], [0, nkb], [1, 8]]), op=ALU.subtract)),
                                reads=[B_negcum, B_nref], writes=[B_btab])

                            s0 = use_slab(l, ti, "qa")
                            yield from proj_fm(s0, 0, 512, 4, lambda j, pb: CP("dve", R[:, j, :], psum[:, pb, 0:NT], [PB[pb]], [B_R[j]]), ring="F")

                        def lru_gen():
                            for j in range(4):
                                pc0 = PR_CW + l * 16
                                cw = lambda t, j=j: prm[:, pc0 + t * 4 + j:pc0 + t * 4 + j + 1]
                                cbj = prm[:, PR_CB + l * 4 + j:PR_CB + l * 4 + j + 1]
                                TS("dve", W[:, 0, :], xcT[:, j, 3:3 + NT], cw(3), cbj, ALU.mult, ALU.add, [B_xc[j], B_prm], [B_W[0]])
                                for t in range(3):
                                    STT("dve", W[:, 0, :], xcT[:, j, t:t + NT], cw(t), W[:, 0, :], ALU.mult, ALU.add, [B_xc[j], B_prm, B_W[0]], [B_W[0]])
                                CP("pool", xcT[:, j, 0:3], xcT[:, j, NT:NT + 3], [B_xc[j]], [B_xc[j]])
                                qq = j % 2
                                CP("pool", cvb[:, qq, :], W[:, 0, :], [B_W[0]], [B_cvb[qq]])
                                yield
                                pr = ps_next("G")
                                MM(psum[:, pr, 0:NT], lruw[:, j * 128:(j + 1) * 128], cvb[:, qq, :], True, True, [B_lruw, B_cvb[qq]], [PB[pr]])
                                pi_ = ps_next("G")
                                MM(psum[:, pi_, 0:NT], lruw[:, (4 + j) * 128:(5 + j) * 128], cvb[:, qq, :], True, True, [B_lruw, B_cvb[qq]], [PB[pi_]])
                                ACT(W[:, 1, :], psum[:, pr, 0:NT], AF.Sigmoid, [PB[pr], B_prm], [B_W[1]], bias=prm[:, PR_BR + l * 4 + j:PR_BR + l * 4 + j + 1])
                                ACT(W[:, 2, :], psum[:, pi_, 0:NT], AF.Sigmoid, [PB[pi_], B_prm], [B_W[2]], bias=prm[:, PR_BI + l * 4 + j:PR_BI + l * 4 + j + 1])
                                yield
                                ACT(W[:, 3, :], W[:, 1, :], AF.Exp, [B_W[1], B_prm], [B_W[3]], scale=ls8[:, l * 4 + j:l * 4 + j + 1])
                                ACT(W[:, 4, :], W[:, 1, :], AF.Exp, [B_W[1], B_prm], [B_W[4]], scale=ls16[:, l * 4 + j:l * 4 + j + 1])
                                ACT(W[:, 4, :], W[:, 4, :], AF.Ln, [B_W[4]], [B_W[4]], bias=1.0, scale=-(1.0 - 1e-7))
                                ACT(W[:, 4, :], W[:, 4, :], AF.Exp, [B_W[4]], [B_W[4]], scale=0.5)
                                TT("dve", W[:, 2, :], W[:, 2, :], W[:, 0, :], ALU.mult, [B_W[2], B_W[0]], [B_W[2]])
                                TT("dve", W[:, 2, :], W[:, 2, :], W[:, 4, :], ALU.mult, [B_W[2], B_W[4]], [B_W[2]])
                                P.add("dve", (lambda e, j=j: e.tensor_tensor_scan(out=W[:, 1, :], data0=W[:, 3, :], data1=W[:, 2, :],
                                                                                   initial=hstate[:, j:j + 1], op0=ALU.mult, op1=ALU.add)),
                                      reads=[B_W[3], B_W[2], B_hst8], writes=[B_W[1]])
                                CP("dve", hstate[:, j:j + 1], W[:, 1, NT - 1:NT], [B_W[1]], [B_hst8])
                                TT("dve", R[:, 16 + j, :], W[:, 1, :], R[:, 20 + j, :], ALU.mult, [B_W[1], B_R[20 + j]], [B_R[16 + j]])
                                yield

                        def interleave(mg, fg_, ratio):
                            mdone = fdone_ = False
                            cnt_ = 0
                            while not (mdone and fdone_):
                                if not mdone:
                                    try:
                                        next(mg)
                                    except StopIteration:
                                        mdone = True
                                cnt_ += 1
                                if not fdone_ and (mdone or cnt_ % ratio == 0):
                                    try:
                                        next(fg_)
                                    except StopIteration:
                                        fdone_ = True

                        interleave(parta_gen(), lru_gen(), 1)
                        def fox_gen():
                            for c in range(4):
                                accs = (6, 7)
                                pend = []

                                def pv_pair(item):
                                    kb_, qs_, n_, pis = item
                                    for hh in range(2):
                                        vo = c * 192 + (0 if hh == 0 else 64)
                                        MM(psum[:, accs[hh], qs_:NT], vfx[:, kb_, vo:vo + 128], pT[:, pis[hh], 0:n_], kb_ == 0, kb_ == nkb - 1,
                                           [B_vf[kb_], B_vones, B_pT[pis[hh]]], [PB[accs[hh]]])

                                for kb in range(nkb):
                                    qs = max(0, kb - b0) * 128
                                    n = NT - qs
                                    pbs = (ps_next("F"), ps_next("F"))
                                    for hh in range(2):
                                        po = hh * 64
                                        MM(psum[:, pbs[hh], 0:n], kTf[po:po + 64, c, kb * 128:(kb + 1) * 128], R[po:po + 64, 4 + c, qs:NT], True, True,
                                           [B_kTf[c][kb // NBT], B_R[4 + c]], [PB[pbs[hh]]])
                                    pis = (pt_next(), pt_next())
                                    for hh in range(2):
                                        h = 2 * c + hh
                                        ACT(pT[:, pis[hh], 0:n], psum[:, pbs[hh], 0:n], AF.Exp, [PB[pbs[hh]], B_btab], [B_pT[pis[hh]]], bias=btab[:, kb, h:h + 1])
                                        if kb >= b0:
                                            TT("pool", pT[:, pis[hh], 0:128], pT[:, pis[hh], 0:128], trib[:, :], ALU.mult, [B_pT[pis[hh]], B_const], [B_pT[pis[hh]]])
                                    pend.append((kb, qs, n, pis))
                                    if len(pend) > 1:
                                        pv_pair(pend.pop(0))
                                    yield
                                while pend:
                                    pv_pair(pend.pop(0))
                                for hh in range(2):
                                    q = hh
                                    dlo, olo = (64, 0) if hh == 0 else (0, 64)
                                    RECIP(rd[dlo:dlo + 64, q, 0:NT], psum[dlo:dlo + 64, accs[hh], 0:NT], [PB[accs[hh]]], [B_rd[q]])
                                    TT("dve", R[olo:olo + 64, 12 + c, :], psum[olo:olo + 64, accs[hh], 0:NT], rd[dlo:dlo + 64, q, 0:NT], ALU.mult,
                                       [PB[accs[hh]], B_rd[q]], [B_R[12 + c]])
                                yield

                        def filler_gen():
                            SL2HH = [0, 2, 1, 3]
                            for bb in range(NBT):
                                b = b0 + bb
                                qc0 = bb * 128
                                for g in range(2):
                                    acc = 4
                                    kinds = [1] if b == 0 else [0, 1]
                                    pvl = []
                                    for kidx, kind in enumerate(kinds):
                                        pb = 5
                                        kc0 = bb * 128 if kind == 0 else (bb + 1) * 128
                                        for slot in range(4):
                                            hh = SL2HH[slot]
                                            MM(psum[:, pb, slot * 128:(slot + 1) * 128], kTa[g * 64:(g + 1) * 64, kc0:kc0 + 128],
                                               R[g * 64:(g + 1) * 64, hh, qc0:qc0 + 128], True, True, [B_kTa, B_R[hh]], [PB[pb]])
                                        STT("dve", swt[:, 0, :], psum[:, pb, :], 0.125, swab[:, g * 2 + kind, :], ALU.mult, ALU.add,
                                            [PB[pb], B_swab], [B_swt[0]])
                                        pi = 4 + (st["pts"] % 2)
                                        st["pts"] += 1
                                        ACT(pT[:, pi, :], swt[:, 0, :], AF.Exp, [B_swt[0]], [B_pT[pi]])
                                        vs = bb if kind == 0 else bb + 1
                                        pvl.append((vs, pi))
                                    yield
                                    for kidx, (vs, pi) in enumerate(pvl):
                                        MM(psum[:, acc, 0:256], vsw[:, vs, 64 + g * 128:192 + g * 128], pT[:, pi, 0:256], kidx == 0, kidx == len(pvl) - 1,
                                           [B_vsw[vs], B_pT[pi]], [PB[acc]])
                                    for kidx, (vs, pi) in enumerate(pvl):
                                        MM(psum[:, acc, 256:512], vsw[:, vs, g * 128:128 + g * 128], pT[:, pi, 256:512], kidx == 0, kidx == len(pvl) - 1,
                                           [B_vsw[vs], B_pT[pi]], [PB[acc]])
                                    q = g
                                    for slot in range(4):
                                        h = 4 * g + SL2HH[slot]
                                        cs = slice(slot * 128, (slot + 1) * 128)
                                        dlo, olo = (64, 0) if slot < 2 else (0, 64)
                                        TS("dve", rd[dlo:dlo + 64, q, cs], psum[dlo:dlo + 64, acc, cs], es[dlo:dlo + 64, l * 8 + h:l * 8 + h + 1], None,
                                           ALU.add, None, [PB[acc], B_prm], [B_rd[q]])
                                    for half in range(2):
                                        dlo = 64 if half == 0 else 0
                                        cs2 = slice(half * 256, half * 256 + 256)
                                        RECIP(rd[dlo:dlo + 64, q, cs2], rd[dlo:dlo + 64, q, cs2], [B_rd[q]], [B_rd[q]])
                                    for slot in range(4):
                                        h = 4 * g + SL2HH[slot]
                                        cs = slice(slot * 128, (slot + 1) * 128)
                                        dlo, olo = (64, 0) if slot < 2 else (0, 64)
                                        ch = 8 + h // 2
                                        TT("dve", R[olo:olo + 64, ch, qc0:qc0 + 128], psum[olo:olo + 64, acc, cs], rd[dlo:dlo + 64, q, cs], ALU.mult,
                                           [PB[acc], B_rd[q]], [B_R[ch]])
                                    yield

                        n_fox = 4 * (nkb + 1)
                        n_fill = NBT * 4
                        fg, gg = fox_gen(), filler_gen()
                        ratio = max(1, n_fox // n_fill)
                        fdone = gdone = False
                        cnt = 0
                        while not (fdone and gdone):
                            if not fdone:
                                try:
                                    next(fg)
                                except StopIteration:
                                    fdone = True
                            cnt += 1
                            if not gdone and (fdone or cnt % ratio == 0):
                                try:
                                    next(gg)
                                except StopIteration:
                                    gdone = True

                        if dbg and l == 0 and ti == 0:
                            DMA(dbgo, R[:, 8:20, :], B_R[8:20], [])
                        for m in range(8):
                            sm_ = use_slab(l, ti, "mg%d" % m)
                            wo = 0 if m % 2 == 0 else 3
                            pg = []
                            for x_ in range(3):
                                pb = ps_next()
                                for k in range(KC):
                                    MM(psum[:, pb, 0:NT], wv(sm_, x_ * 128, 384, k, 0, 128), uT[:, k, :], k == 0, k == KC - 1,
                                       [B_wr[sm_], B_uT[k]], [PB[pb]])
                                ACT(W[:, wo + x_, :], psum[:, pb, 0:NT], AF.Sigmoid, [PB[pb]], [B_W[wo + x_]])
                            for x_ in range(3):
                                pb = ps_next()
                                for kk in range(4):
                                    MM(psum[:, pb, 0:NT], wv(sm_, 3072 + x_ * 128, 384, kk, 0, 128), R[:, 8 + 4 * x_ + kk, :], kk == 0, kk == 3,
                                       [B_wr[sm_], B_R[8 + 4 * x_ + kk]], [PB[pb]])
                                TT("dve", W[:, wo + x_, :], W[:, wo + x_, :], psum[:, pb, 0:NT], ALU.mult, [B_W[wo + x_], PB[pb]], [B_W[wo + x_]])
                            TT("pool", W[:, wo, :], W[:, wo, :], W[:, wo + 1, :], ALU.add, [B_W[wo], B_W[wo + 1]], [B_W[wo]])
                            TT("pool", R[:, m, :], W[:, wo, :], W[:, wo + 2, :], ALU.add, [B_W[wo], B_W[wo + 2]], [B_R[m]])
                        for hf in range(2):
                            so_ = use_slab(l, ti, "wo%d" % hf)
                            for mm in range(4):
                                m = hf * 4 + mm
                                pb = ps_next()
                                for k in range(KC):
                                    MM(psum[:, pb, 0:NT], wv(so_, 0, 512, k, mm * 128, 128), R[:, k, :], k == 0, k == KC - 1,
                                       [B_wr[so_], B_R[k]], [PB[pb]])
                                TT("dve", hT[:, hb, m, :], hT[:, hb, m, :], psum[:, pb, 0:NT], ALU.add, [B_hT[hb][m], PB[pb]], [B_hT[hb][m]])
                        rmsnorm_to_uT(hb, PR_NFFN + l * 8)
                        for s_ in range(11):
                            sf = use_slab(l, ti, "fi%d" % s_)
                            for pp in range(2):
                                j = 2 * s_ + pp
                                pbg = ps_next()
                                for k in range(KC):
                                    MM(psum[:, pbg, 0:NT], wv(sf, 0, 512, k, (2 * pp) * 128, 128), uT[:, k, :], k == 0, k == KC - 1,
                                       [B_wr[sf], B_uT[k]], [PB[pbg]])
                                pbu = ps_next()
                                for k in range(KC):
                                    MM(psum[:, pbu, 0:NT], wv(sf, 0, 512, k, (2 * pp + 1) * 128, 128), uT[:, k, :], k == 0, k == KC - 1,
                                       [B_wr[sf], B_uT[k]], [PB[pbu]])
                                wq = j % 6
                                ACT(W[:, wq, :], psum[:, pbg, 0:NT], AF.Silu, [PB[pbg]], [B_W[wq]])
                                TT("dve", R[:, j, :], W[:, wq, :], psum[:, pbu, 0:NT], ALU.mult, [B_W[wq], PB[pbu]], [B_R[j]])
                        for m in range(8):
                            sf = use_slab(l, ti, "fo%d" % m)
                            pb = ps_next()
                            for k in range(FK):
                                MM(psum[:, pb, 0:NT], wv(sf, 0, 128, k, 0, 128), R[:, k, :], k == 0, k == FK - 1,
                                   [B_wr[sf], B_R[k]], [PB[pb]])
                            TT("dve", hT[:, hb, m, :], hT[:, hb, m, :], psum[:, pb, 0:NT], ALU.add, [B_hT[hb][m], PB[pb]], [B_hT[hb][m]])
                        if not last_layer:
                            DMA(hbuf[:, :, t0:t0 + NT], hT[:, hb, :, :], B_hT[hb], [B_hb[ti]])
                        else:
                            ssb = ps_next()
                            for c in range(KC):
                                q = c % 2
                                ACT(sq[:, q, :], hT[:, hb, c, :], AF.Square, [B_hT[hb][c]], [B_sq[q]])
                                MM(psum[:, ssb, 0:NT], onesb[:, :], sq[:, q, :], c == 0, c == KC - 1, [B_const, B_sq[q]], [PB[ssb]])
                            ACT(rstd[:, :], psum[:, ssb, 0:NT], AF.Ln, [PB[ssb]], [B_rstd], bias=epsb[:, 0:1], scale=1.0 / D)
                            ACT(rstd[:, :], rstd[:, :], AF.Exp, [B_rstd], [B_rstd], scale=-0.5)
                            for c in range(KC):
                                eng = "dve" if c % 2 == 0 else "pool"
                                STT(eng, hT[:, hb, c, :], hT[:, hb, c, :], prm[:, PR_NFIN + c:PR_NFIN + c + 1], rstd[:, :], ALU.mult, ALU.mult,
                                    [B_hT[hb][c], B_prm, B_rstd], [B_hT[hb][c]])
                            for bb in range(NBT):
                                b = b0 + bb
                                oq = b % 2
                                for half in range(2):
                                    pb = ps_next()
                                    for cc in range(4):
                                        c = half * 4 + cc
                                        P.add("pe", (lambda e, pb=pb, cc=cc, c=c, bb=bb, hb=hb: e.transpose(
                                            psum[:, pb, cc * 128:(cc + 1) * 128], hT[:, hb, c, bb * 128:(bb + 1) * 128], identf[:])),
                                            reads=[B_hT[hb][c], B_const], writes=[PB[pb]])
                                    CP("act" if half else "dve", ost[oq][:, half * 512:(half + 1) * 512], psum[:, pb, :], [PB[pb]], B_ost[oq])
                                if b == 0:
                                    DMA(out[0:112, :], ost[oq][16:128, :], B_ost[oq], [])
                                elif b == NB - 1:
                                    DMA(out[SEQ - 16:SEQ, :], ost[oq][0:16, :], B_ost[oq], [])
                                else:
                                    DMA(out[128 * b - 16:128 * b + 112, :], ost[oq][:, :], B_ost[oq], [])
                fin = P.add("sp", None, reads=[], writes=[])
                fin.deps = [o for e_ in ENGS for o in P.ops[e_] if o.is_dma]
                P.emit(nc, sems, dsems, base)
    return nc


def make_consts():
    c = np.zeros((128, 384), np.float32)
    c[:, 0:128] = np.eye(128, dtype=np.float32)
    c[:, 128:256] = np.triu(np.ones((128, 128), np.float32))
    import math
    for dist in range(128):
        if dist < 16:
            b = dist
        else:
            sc = np.log(np.float32(max(dist, 1)) / np.float32(16)) / np.float32(math.log(128 / 16))
            b = min(16 + int(np.float32(sc) * np.float32(16)), 31)
        c[b, 256 + dist] = 1.0
    return c


_NC_CACHE = {}


def kernel(**inputs):
    if "nc" not in _NC_CACHE:
        _NC_CACHE["nc"] = build()
    nc = _NC_CACHE["nc"]
    cst = make_consts()
    xs = np.ascontiguousarray(inputs["x"], dtype=np.float32)
    shared = {k: np.ascontiguousarray(v, dtype=np.float32) for k, v in inputs.items() if k != "x"}
    shared["consts"] = cst
    in_maps = []
    for i in range(8):
        m = dict(shared)
        m["x"] = xs[i]
        in_maps.append(m)
    res = run_bass_kernel_spmd(nc, in_maps, core_ids=list(range(8)))
    return np.stack([np.asarray(r["out"], dtype=np.float32) for r in res.results], axis=0)
```

```python
import numpy as np
import concourse.bass as bass
import concourse.mybir as mybir
from concourse.bass_utils import run_bass_kernel_spmd

F32 = mybir.dt.float32
BF16 = mybir.dt.bfloat16
AF = mybir.ActivationFunctionType
ALU = mybir.AluOpType

D = 1024
KC = 8
L_FULL = 4
SEQ = 4096
NMETA = 16
NB = 33
TP = NB * 128
NBT = 3
NT = NBT * 128
NTILES_FULL = NB // NBT
INC = 6408
DFF = 2816
FK = DFF // 128
QA, KA, VA, QF, KF, VF, FL, XC, YC, GT = 0, 512, 640, 768, 1280, 1792, 2304, 2312, 2824, 3336
SLOT = 4608
NRING = 3
ND = 40
NDP = 16
NEG = -30000.0
EPS = 1e-6
VW = 768
ENGS = ("pe", "act", "dve", "pool", "sp")


ALLBUFS = []


class Buf:
    __slots__ = ("w", "r", "name")

    def __init__(self, name=""):
        self.w = None
        self.r = []
        self.name = name
        ALLBUFS.append(self)


class Op:
    __slots__ = ("eng", "fn", "deps", "signal", "sigval", "is_dma", "didx")


class Prog:
    def __init__(self):
        self.ops = {e: [] for e in ENGS}
        self.ndma = 0
        self.ndma_p = 0

    def add(self, eng, fn, reads=(), writes=(), dma=False):
        op = Op()
        op.eng, op.fn, op.signal, op.sigval, op.is_dma, op.didx = eng, fn, False, 0, dma, -1
        deps = set()
        for b in reads:
            if b.w is not None:
                deps.add(b.w)
        for b in writes:
            if b.w is not None:
                deps.add(b.w)
            deps.update(b.r)
        for b in reads:
            if not dma:
                b.r = [o for o in b.r if o.is_dma or o.eng != eng]
            b.r.append(op)
        for b in writes:
            b.w = op
            b.r = []
        op.deps = [d for d in deps if d.is_dma or not (d.eng == "pe" and eng == "pe")]
        for d in op.deps:
            d.signal = True
        if dma:
            if eng == "pool":
                op.didx = self.ndma_p
                self.ndma_p += 1
            else:
                op.didx = self.ndma
                self.ndma += 1
        self.ops[eng].append(op)
        return op

    def barrier(self):
        lasts = [self.ops[e][-1] for e in ENGS if self.ops[e] and not self.ops[e][-1].is_dma]
        for e in ("pe", "act", "dve", "pool"):
            for o in reversed(self.ops[e]):
                if o.fn is not None:
                    lasts.append(o)
                    break
        dmas = [o for e_ in ENGS for o in self.ops[e_] if o.is_dma]
        for e in ENGS:
            op = self.add(e, None)
            op.deps = list(set(lasts)) + dmas
            for d in op.deps:
                d.signal = True
        for b in ALLBUFS:
            b.w = None
            b.r = []

    def emit(self, nc, sems, dsems, base):
        for e in ENGS:
            cnt = base.get(("e", e), 0)
            for op in self.ops[e]:
                if op.is_dma:
                    continue
                if op.signal:
                    cnt += 1
                op.sigval = cnt
            base[("e", e)] = cnt
        d0 = base.get("dma", 0)

        def dkey(d):
            if d.eng == "pool":
                return ("q", d.didx % NDP), 16 * (d.didx // NDP + 1)
            gi = d.didx + d0
            return ("d", gi % ND), 16 * (gi // ND + 1)

        def run(e, eng):
            known = {}
            for op in self.ops[e]:
                waits = {}
                for d in op.deps:
                    if d.is_dma:
                        key, val = dkey(d)
                    else:
                        key, val = ("e", d.eng), d.sigval
                    if waits.get(key, 0) < val:
                        waits[key] = val
                if op.is_dma:
                    key, val = dkey(op)
                    val -= 16
                    if val > 0 and waits.get(key, 0) < val:
                        waits[key] = val
                for key, val in waits.items():
                    if known.get(key, 0) >= val:
                        continue
                    s = dsems[key[1]] if key[0] == "d" else (dsems[ND + key[1]] if key[0] == "q" else sems[key[1]])
                    eng.wait_ge(s, val)
                    known[key] = val
                if op.fn is None:
                    continue
                ins = op.fn(eng)
                if op.is_dma:
                    k_ = dkey(op)[0]
                    ins.then_inc(dsems[k_[1]] if k_[0] == "d" else dsems[ND + k_[1]], 16)
                elif op.signal:
                    ins.then_inc(sems[e], 1)

        with nc.Block() as block:
            @block.tensor
            def _(eng):
                run("pe", eng)

            @block.scalar
            def _(eng):
                run("act", eng)

            @block.vector
            def _(eng):
                run("dve", eng)

            @block.gpsimd
            def _(eng):
                run("pool", eng)

            @block.sync
            def _(eng):
                run("sp", eng)
        base["dma"] = d0 + self.ndma


def slab_defs():
    sl = []
    sl.append(("xc", 4096, [("w_in", 0, 8, XC, 512, 0, 512)]))
    sl.append(("yc", 4096, [("w_in", 0, 8, YC, 512, 0, 512)]))
    sl.append(("qf", 4096, [("w_in", 0, 8, QF, 512, 0, 512)]))
    sl.append(("kf", 4096, [("w_in", 0, 8, KF, 512, 0, 512)]))
    sl.append(("vf", 4096, [("w_in", 0, 8, VF, 512, 0, 512)]))
    sl.append(("kav", 8 * 264, [("w_in", 0, 8, KA, 128, 0, 264), ("w_in", 0, 8, VA, 128, 128, 264),
                                ("w_in", 0, 8, FL, 8, 256, 264)]))
    pc = []
    for c in range(4):
        pc.append(("w_in", 0, 8, QA + c * 64, 64, c * 128, 512))
        pc.append(("w_in", 0, 8, QA + (c + 4) * 64, 64, c * 128 + 64, 512))
    sl.append(("qa", 4096, pc))
    for m in range(8):
        pc = []
        for x in range(3):
            pc.append(("w_in", 0, 8, GT + x * 1024 + m * 128, 128, x * 128, 384))
        for x in range(3):
            pc.append(("w_branch%d" % x, 0, 4, m * 128, 128, 3072 + x * 128, 384))
        sl.append(("mg%d" % m, 3072 + 1536, pc))
    for hf in range(2):
        sl.append(("wo%d" % hf, 4096, [("w_out", 0, 8, hf * 512, 512, 0, 512)]))
    for s in range(11):
        pc = []
        for pp in range(2):
            j = 2 * s + pp
            pc.append(("w_ffn_in", 0, 8, j * 128, 128, (2 * pp) * 128, 512))
            pc.append(("w_ffn_in", 0, 8, DFF + j * 128, 128, (2 * pp + 1) * 128, 512))
        sl.append(("fi%d" % s, 4096, pc))
    for m in range(8):
        sl.append(("fo%d" % m, FK * 128, [("w_ffn_out", 0, FK, m * 128, 128, 0, 128)]))
    return sl


SLABS = slab_defs()
NSLAB = len(SLABS)
SLAB_IDX = {s[0]: i for i, s in enumerate(SLABS)}
PR_NMIX, PR_NFFN, PR_NFIN, PR_CW, PR_CB, PR_BR, PR_BI, PR_LAM = 0, 32, 64, 72, 136, 152, 168, 184
NPRM = 200


def build(n_layers=L_FULL, n_tiles=NTILES_FULL, dbg=False):
    nc = bass.Bass("TRN2", target_bir_lowering=False)
    dt_in = lambda n, s: nc.dram_tensor(n, s, F32, kind="ExternalInput").ap()
    x = dt_in("x", [SEQ, D])
    meta = dt_in("meta_tokens", [NMETA, D])
    relb = dt_in("rel_bias_table", [32, 8])
    norm_mix = dt_in("norm_mix", [L_FULL, D])
    w_in = dt_in("w_in", [L_FULL, D, INC])
    sinks = dt_in("swa_sinks", [L_FULL, 8])
    fbias = dt_in("fox_forget_bias", [L_FULL, 8])
    conv_w = dt_in("conv_w", [L_FULL, 4, 512])
    conv_b = dt_in("conv_b", [L_FULL, 512])
    lru_w_r = dt_in("lru_w_r", [L_FULL, 8, 64, 64])
    lru_b_r = dt_in("lru_b_r", [L_FULL, 512])
    lru_w_i = dt_in("lru_w_i", [L_FULL, 8, 64, 64])
    lru_b_i = dt_in("lru_b_i", [L_FULL, 512])
    lru_lam = dt_in("lru_lambda", [L_FULL, 512])
    w_branch = dt_in("w_branch", [L_FULL, 3, 512, D])
    w_out = dt_in("w_out", [L_FULL, D, D])
    norm_ffn = dt_in("norm_ffn", [L_FULL, D])
    w_ffn_in = dt_in("w_ffn_in", [L_FULL, D, 2 * DFF])
    w_ffn_out = dt_in("w_ffn_out", [L_FULL, DFF, D])
    norm_fin = dt_in("norm_final", [D])
    cst = dt_in("consts", [128, 384])
    out = nc.dram_tensor("out", [SEQ, D], F32, kind="ExternalOutput").ap()
    if dbg:
        dbgo = nc.dram_tensor("dbgo", [128, 12, NT], BF16, kind="ExternalOutput").ap()
    wbf = nc.dram_tensor("wbf", [n_layers * NSLAB, 128, SLOT], BF16, kind="Internal").ap()
    lrubf = nc.dram_tensor("lrubf", [n_layers, 128, 1024], BF16, kind="Internal").ap()
    hbuf = nc.dram_tensor("hbuf", [128, KC, TP], F32, kind="Internal").ap()
    extd = nc.dram_tensor("extd", [8, 384], F32, kind="Internal").ap()

    srcs = {"w_in": w_in, "w_out": w_out, "w_ffn_in": w_ffn_in, "w_ffn_out": w_ffn_out}

    def wsrc(name, l):
        if name.startswith("w_branch"):
            return w_branch[l, int(name[-1])]
        return srcs[name][l]

    sb = lambda n, s, d: nc.sbuf_tensor(n, s, d)
    import contextlib
    if True:
        with contextlib.ExitStack() as stk:
            identf = stk.enter_context(sb("identf", [128, 128], F32))
            trif = stk.enter_context(sb("trif", [128, 128], F32))
            onesf = stk.enter_context(sb("onesf", [128, 128], F32))
            trib = stk.enter_context(sb("trib", [128, 128], BF16))
            prm = stk.enter_context(sb("prm", [128, NPRM], F32))
            ls8 = stk.enter_context(sb("ls8", [128, 16], F32))
            ls16 = stk.enter_context(sb("ls16", [128, 16], F32))
            es = stk.enter_context(sb("es", [128, 32], F32))
            fb = stk.enter_context(sb("fb", [128, 32], F32))
            epsb = stk.enter_context(sb("epsb", [128, 1], F32))
            onesb = stk.enter_context(sb("onesb", [128, 128], BF16))
            swab = stk.enter_context(sb("swab", [128, 4, 512], F32))
            psum = stk.enter_context(nc.psum_tensor("psum", [128, 8, 512], F32))
            s_pe = stk.enter_context(nc.semaphore("s_pe"))
            s_act = stk.enter_context(nc.semaphore("s_act"))
            s_dve = stk.enter_context(nc.semaphore("s_dve"))
            s_pool = stk.enter_context(nc.semaphore("s_pool"))
            s_sp = stk.enter_context(nc.semaphore("s_sp"))
            kTf = stk.enter_context(sb("kTf", [128, 4, TP], BF16))
            vfx = stk.enter_context(sb("vf", [128, NB, VW], BF16))
            negcum = stk.enter_context(sb("negcum", [128, NB, 8], F32))
            btab = stk.enter_context(sb("btab", [128, NB, 8], F32))
            hT = stk.enter_context(sb("hT", [128, 2, KC, NT], F32))
            uT = stk.enter_context(sb("uT", [128, KC, NT], BF16))
            wr = stk.enter_context(sb("wr", [128, NRING, SLOT], BF16))
            R = stk.enter_context(sb("R", [128, 24, NT], BF16))
            W = stk.enter_context(sb("W", [128, 6, NT], F32))
            xcT = stk.enter_context(sb("xcT", [128, 4, NT + 3], F32))
            rstd = stk.enter_context(sb("rstd", [128, NT], F32))
            pT = stk.enter_context(sb("pT", [128, 6, 512], BF16))
            kTa = stk.enter_context(sb("kTa", [128, 128 + NT], BF16))
            vsw = stk.enter_context(sb("vsw", [128, NBT + 1, 320], BF16))
            swt = stk.enter_context(sb("swt", [128, 1, 512], F32))
            rd = stk.enter_context(sb("rd", [128, 2, 512], F32))
            lruw = stk.enter_context(sb("lruw", [128, 1024], BF16))
            cvb = stk.enter_context(sb("cvb", [128, 2, NT], BF16))
            sm = stk.enter_context(sb("sm", [128, 4, 8], F32))
            rrun = stk.enter_context(sb("rrun", [128, 8], F32))
            nref = stk.enter_context(sb("nref", [128, 8], F32))
            hstate = stk.enter_context(sb("hstate", [128, 4], F32))
            dsems = [stk.enter_context(nc.semaphore("s_d%d" % i)) for i in range(ND + NDP)]
            sems = {"pe": s_pe, "act": s_act, "dve": s_dve, "pool": s_pool, "sp": s_sp}
            base = {}
            B_const = Buf("const")
            B_swab = Buf("swab")
            B_prm = Buf("prm")
            B_hb = [Buf("hb%d" % i) for i in range(NTILES_FULL)]
            B_wbf = [Buf("wbf%d" % i) for i in range(n_layers * NSLAB)]
            B_lrubf = Buf("lrubf")
            PB = [Buf("ps%d" % i) for i in range(8)]

            P = Prog()
            if True:
                stg = vfx[:].rearrange("p a b -> p (a b)").bitcast(F32)[:, 0:2 * SLOT].rearrange("p (a b) -> p a b", a=2)
                stb = kTf[:].rearrange("p a b -> p (a b)")[:, 0:2 * SLOT].rearrange("p (a b) -> p a b", a=2)
                xin = hT[:].rearrange("p a b c -> p (a b c)")[:, 0:2 * D].rearrange("p (a b) -> p a b", a=2)
                hst = W[:].rearrange("p a b -> p (a b)")[:, 0:2 * KC * 128].rearrange("p (a c t) -> p a c t", a=2, c=KC)
                prow = rstd[:, 0:256].rearrange("p (a b) -> p a b", a=2)
                tbl = rstd[0:32, 256:264]
                oh = swt[0:32, 0, 0:128]
                ext = swt[0:8, 0, 128:512]
                lst = uT[:].rearrange("p a b -> p (a b)").bitcast(F32)[:, 0:1024]
                lsb = R[:].rearrange("p a b -> p (a b)")[:, 0:1024]
                tmp16 = sm[:].rearrange("p a b -> p (a b)")[:, 0:16]
                swraw = wr[:].rearrange("p a b -> p (a b)").bitcast(F32)[:, 0:2048].rearrange("p (h q) -> p h q", h=8)
                stg_flat = vfx[:].rearrange("p a b -> p (a b)").bitcast(F32)
                B_stg = [Buf(), Buf()]
                B_stb = [Buf(), Buf()]
                B_xin = [Buf(), Buf()]
                B_hst = [Buf(), Buf()]
                B_prow = Buf()
                B_misc = Buf()
                B_ext = Buf()
                B_extd = Buf()
                B_lst = Buf()
                B_lsb = Buf()
                P.add("sp", lambda e: e.dma_start(out=identf[:], in_=cst[:, 0:128]), writes=[B_const], dma=True)
                P.add("sp", lambda e: e.dma_start(out=trif[:], in_=cst[:, 128:256]), writes=[B_const], dma=True)
                P.add("sp", lambda e: e.dma_start(out=oh[:], in_=cst[0:32, 256:384]), writes=[B_misc], dma=True)
                P.add("sp", lambda e: e.dma_start(out=tbl[:], in_=relb), writes=[B_misc], dma=True)
                P.add("pool", lambda e: e.memset(onesf[:], 1.0), writes=[B_const])
                P.add("pool", lambda e: e.memset(epsb[:], EPS), writes=[B_const])
                P.add("pool", lambda e: e.memset(onesb[:], 1.0), writes=[B_const])
                P.add("dve", lambda e: e.tensor_copy(out=trib[:], in_=trif[:]), reads=[B_const], writes=[B_const])
                rows = []
                rows += [(PR_NMIX, norm_mix.rearrange("l (c p) -> (l c) p", p=128), 32)]
                rows += [(PR_NFFN, norm_ffn.rearrange("l (c p) -> (l c) p", p=128), 32)]
                rows += [(PR_NFIN, norm_fin.rearrange("(c p) -> c p", p=128), 8)]
                rows += [(PR_CW, conv_w.rearrange("l t (c p) -> (l t c) p", p=128), 64)]
                rows += [(PR_CB, conv_b.rearrange("l (c p) -> (l c) p", p=128), 16)]
                rows += [(PR_BR, lru_b_r.rearrange("l (c p) -> (l c) p", p=128), 16)]
                rows += [(PR_BI, lru_b_i.rearrange("l (c p) -> (l c) p", p=128), 16)]
                rows += [(PR_LAM, lru_lam.rearrange("l (c p) -> (l c) p", p=128), 16)]
                B_prow_l = [Buf() for _ in range(24)]
                prow_i = [0]
                P.add("pool", lambda e: e.memset(prow[:], 0.0), writes=B_prow_l)
                for (r0, src, n) in rows:
                    a = r0
                    while a < r0 + n:
                        g = a // 128
                        b_ = min(r0 + n, (g + 1) * 128)
                        P.add("sp", (lambda e, g=g, a=a, b_=b_, src=src, r0=r0: e.dma_start(
                            out=prow[a - g * 128:b_ - g * 128, g, :], in_=src[a - r0:b_ - r0, :])),
                            writes=[B_prow_l[prow_i[0]]], dma=True)
                        prow_i[0] += 1
                        a = b_
                for g in range(2):
                    ncol = min(128, NPRM - g * 128)
                    P.add("pe", (lambda e, g=g: e.transpose(psum[:, g, 0:128], prow[:, g, :], identf[:])),
                          reads=B_prow_l + [B_const], writes=[PB[g]])
                    P.add("dve", (lambda e, g=g, ncol=ncol: e.tensor_copy(out=prm[:, g * 128:g * 128 + ncol],
                                                                      in_=psum[:, g, 0:ncol])),
                          reads=[PB[g]], writes=[B_prm])
                P.add("act", lambda e: e.activation(out=tmp16[:], in_=prm[:, PR_LAM:PR_LAM + 16], func=AF.Exp, scale=-1.0),
                      reads=[B_prm], writes=[B_misc])
                P.add("act", lambda e: e.activation(out=tmp16[:], in_=tmp16[:], func=AF.Ln, bias=1.0),
                      reads=[B_misc], writes=[B_misc])
                P.add("dve", lambda e: e.tensor_scalar(out=ls8[:], in0=tmp16[:], scalar1=-8.0, scalar2=None, op0=ALU.mult),
                      reads=[B_misc], writes=[B_prm])
                P.add("dve", lambda e: e.tensor_scalar(out=ls16[:], in0=tmp16[:], scalar1=-16.0, scalar2=None, op0=ALU.mult),
                      reads=[B_misc], writes=[B_prm])
                P.add("sp", lambda e: e.dma_start(out=es[:], in_=sinks.rearrange("l h -> (l h)").partition_broadcast(128)),
                      writes=[B_misc], dma=True)
                P.add("sp", lambda e: e.dma_start(out=fb[:], in_=fbias.rearrange("l h -> (l h)").partition_broadcast(128)),
                      writes=[B_prm], dma=True)
                P.add("act", lambda e: e.activation(out=es[:], in_=es[:], func=AF.Exp), reads=[B_misc], writes=[B_prm])
                P.add("pe", lambda e: e.matmul(psum[0:8, 2, 0:128], lhsT=tbl[:, :], rhs=oh[:, :], start=True, stop=True),
                      reads=[B_misc], writes=[PB[2]])
                P.add("pool", lambda e: e.memset(ext[:], NEG), writes=[B_ext])
                P.add("dve", lambda e: e.tensor_copy(out=ext[:, 128:256], in_=psum[0:8, 2, 0:128]),
                      reads=[PB[2], B_ext], writes=[B_ext])
                P.add("sp", lambda e: e.dma_start(out=extd, in_=ext[:]), reads=[B_ext], writes=[B_extd], dma=True)
                SLOT2HH = [0, 2, 1, 3]
                B_swraw_l = [Buf() for _ in range(128)]
                for ki in range(128):
                    P.add("sp", (lambda e, ki=ki: e.dma_start(out=swraw[ki:ki + 1, :, :],
                                                              in_=extd[:, 128 - ki:384 - ki])),
                          reads=[B_extd], writes=[B_swraw_l[ki]], dma=True)
                for g in range(2):
                    for kind in range(2):
                        for slot in range(4):
                            h = 4 * g + SLOT2HH[slot]
                            so = 128 if kind == 0 else 0
                            P.add("dve", (lambda e, g=g, kind=kind, slot=slot, h=h, so=so: e.tensor_copy(
                                out=swab[:, g * 2 + kind, slot * 128:(slot + 1) * 128], in_=swraw[:, h, so:so + 128])),
                                reads=B_swraw_l, writes=[B_swab])
                for l in range(n_layers):
                    B_lst_l = [Buf() for _ in range(16)]
                    P.add("pool", lambda e: e.memset(lst[:], 0.0), reads=[B_lst], writes=B_lst_l)
                    for ri, wsrc_ in enumerate((lru_w_r, lru_w_i)):
                        for j in range(4):
                            for hb_ in range(2):
                                blk = 2 * j + hb_
                                P.add("sp", (lambda e, l=l, ri=ri, j=j, hb_=hb_, blk=blk, wsrc_=wsrc_: e.dma_start(
                                    out=lst[hb_ * 64:(hb_ + 1) * 64, (ri * 4 + j) * 128 + hb_ * 64:(ri * 4 + j) * 128 + hb_ * 64 + 64],
                                    in_=wsrc_[l, blk])), writes=[B_lst_l[ri * 8 + j * 2 + hb_]], dma=True)
                    P.add("dve", lambda e: e.tensor_copy(out=lsb[:], in_=lst[:]), reads=B_lst_l, writes=[B_lsb, B_lst])
                    P.add("sp", (lambda e, l=l: e.dma_start(out=lrubf[l], in_=lsb[:])), reads=[B_lsb],
                          writes=[B_lrubf], dma=True)
                def conv_dmas(l):
                    out_ = []
                    for si, (nm, nel, pieces) in enumerate(SLABS):
                        idx = l * NSLAB + si
                        for pi_, (src, row0, nk, col0, ncols, doff, dks) in enumerate(pieces):
                            Wsrc = wsrc(src, l)
                            import os
                            if ncols < int(os.environ.get('MINCOLS', '0')):
                                continue
                            for k0 in range(0, nk, 8):
                                k1 = min(nk, k0 + 8)
                                sap = Wsrc[row0 + k0 * 128:row0 + k1 * 128, col0:col0 + ncols].rearrange("(k p) c -> p k c", p=128)
                                dap = bass.AP(wbf.tensor, idx * 128 * SLOT + doff + k0 * dks, [[SLOT, 128], [dks, k1 - k0], [1, ncols]])
                                out_.append((idx, sap, dap))
                    return out_

                def rec_conv(item):
                    idx, sap, dap = item
                    op = P.add("pool", (lambda e, sap=sap, dap=dap: e.dma_start(out=dap, in_=sap)), dma=True)
                    CONVW.setdefault(idx, []).append(op)

                CONVW = {}
                P.barrier()
                conv0 = {}
                for item in conv_dmas(0):
                    conv0.setdefault(item[0], []).append(item)

            if True:
                B_kTf = [[Buf() for _ in range(NTILES_FULL)] for _ in range(4)]
                B_vf = [Buf() for _ in range(NB)]
                B_vones = Buf()
                B_negcum = Buf()
                B_btab = Buf()
                B_hT = [[Buf() for _ in range(KC)] for _ in range(2)]
                B_uT = [Buf() for _ in range(KC)]
                B_wr = [Buf() for _ in range(NRING)]
                B_R = [Buf() for _ in range(24)]
                B_W = [Buf() for _ in range(6)]
                B_xc = [Buf() for _ in range(4)]
                B_rstd = Buf()
                B_pT = [Buf() for _ in range(6)]
                B_kTa = Buf()
                B_vsw = [Buf() for _ in range(NBT + 1)]
                B_swt = [Buf(), Buf()]
                B_rd = [Buf(), Buf()]
                B_lruw = Buf()
                B_cvb = [Buf(), Buf()]
                B_sm = [Buf() for _ in range(4)]
                B_rrun = Buf()
                B_nref = Buf()
                B_hst8 = Buf()
                sq = cvb
                B_sq = B_cvb
                Wflat = W[:].rearrange('p a b -> p (a b)')
                ost = [Wflat[:, 0:D], Wflat[:, 3 * NT:3 * NT + D]]
                B_ost = [[B_W[0], B_W[1], B_W[2]], [B_W[3], B_W[4], B_W[5]]]
                st = {"cv": 0, "psA": 0, "psF": 0, "psG": 0, "acc": 0, "pt": 0, "pts": 0, "ld": 0}

                RINGS = {"A": [0, 1, 2, 3, 4, 5], "F": [0, 1, 2, 3], "G": [4, 5]}

                def ps_next(ring="A"):
                    r = RINGS[ring]
                    i = r[st["ps" + ring] % len(r)]
                    st["ps" + ring] += 1
                    return i

                def acc_next():
                    i = 6 + st["acc"] % 2
                    st["acc"] += 1
                    return i

                def pt_next():
                    i = st["pt"] % 4
                    st["pt"] += 1
                    return i

                def MM(o, lt, rh, start, stop, reads, writes):
                    return P.add("pe", (lambda e: e.matmul(o, lhsT=lt, rhs=rh, start=start, stop=stop)), reads=reads, writes=writes)

                def ACT(o, i, func, reads, writes, bias=None, scale=None):
                    kw = {}
                    if bias is not None:
                        kw["bias"] = bias
                    if scale is not None:
                        kw["scale"] = scale
                    return P.add("act", (lambda e: e.activation(out=o, in_=i, func=func, **kw)), reads=reads, writes=writes)

                def TT(eng, o, a, b, op, reads, writes):
                    return P.add(eng, (lambda e: e.tensor_tensor(out=o, in0=a, in1=b, op=op)), reads=reads, writes=writes)

                def TS(eng, o, a, s1, s2, op0, op1, reads, writes):
                    if op1 is None:
                        return P.add(eng, (lambda e: e.tensor_scalar(out=o, in0=a, scalar1=s1, scalar2=None, op0=op0)), reads=reads, writes=writes)
                    return P.add(eng, (lambda e: e.tensor_scalar(out=o, in0=a, scalar1=s1, scalar2=s2, op0=op0, op1=op1)), reads=reads, writes=writes)

                def STT(eng, o, a, s, b, op0, op1, reads, writes):
                    eng = "dve"
                    return P.add(eng, (lambda e: e.scalar_tensor_tensor(out=o, in0=a, scalar=s, in1=b, op0=op0, op1=op1)), reads=reads, writes=writes)

                def RECIP(o, i, reads, writes):
                    return P.add("dve", (lambda e: e.reciprocal(out=o, in_=i)), reads=reads, writes=writes)

                def CP(eng, o, i, reads, writes):
                    if eng == "act":
                        return ACT(o, i, AF.Copy, reads, writes)
                    return P.add(eng, (lambda e: e.tensor_copy(out=o, in_=i)), reads=reads, writes=writes)

                def DMA(o, i, reads, writes):
                    return P.add("sp", (lambda e: e.dma_start(out=o, in_=i)), reads=reads, writes=writes, dma=True)

                total_slabs = n_layers * n_tiles * NSLAB

                def slab_global(l, ti, si):
                    return (l * n_tiles + ti) * NSLAB + si

                def ensure_loaded(upto):
                    upto = min(upto, total_slabs - 1)
                    while st["ld"] <= upto:
                        gidx = st["ld"]
                        if gidx < NSLAB:
                            for ci in range(st["cv"], min(NSLAB, gidx + 7)):
                                for item in conv0.get(ci, []):
                                    rec_conv(item)
                            st["cv"] = max(st["cv"], min(NSLAB, gidx + 7))
                        l = gidx // (n_tiles * NSLAB)
                        si = gidx % NSLAB
                        nel = SLABS[si][1]
                        slot = gidx % NRING
                        op_ = DMA(wr[:, slot, 0:nel], wbf[l * NSLAB + si, :, 0:nel], [B_wbf[l * NSLAB + si]], [B_wr[slot]])
                        for cop in CONVW.get(l * NSLAB + si, []):
                            if cop not in op_.deps:
                                op_.deps.append(cop)
                        st["ld"] += 1

                def use_slab(l, ti, name):
                    g = slab_global(l, ti, SLAB_IDX[name])
                    ensure_loaded(g + NRING - 1)
                    return g % NRING

                def wv(slot, off, kstride, k, c0, ncols):
                    return wr[:, slot, off + k * kstride + c0: off + k * kstride + c0 + ncols]

                P.add("pool", lambda e: e.memset(vfx[:].rearrange("p b (c w) -> p (b c) w", w=192)[:, :, 64:128], 1.0), writes=[B_vones])
                for s_ in range(NBT + 1):
                    P.add("pool", (lambda e, s_=s_: e.memset(vsw[:, s_, :], 1.0)), writes=[B_vsw[s_]])

                def rmsnorm_to_uT(hb, gcol):
                    ssb = ps_next()
                    for c in range(KC):
                        q = c % 2
                        ACT(sq[:, q, :], hT[:, hb, c, :], AF.Square, [B_hT[hb][c]], [B_sq[q]])
                        MM(psum[:, ssb, 0:NT], onesb[:, :], sq[:, q, :], c == 0, c == KC - 1, [B_const, B_sq[q]], [PB[ssb]])
                    ACT(rstd[:, :], psum[:, ssb, 0:NT], AF.Ln, [PB[ssb]], [B_rstd], bias=epsb[:, 0:1], scale=1.0 / D)
                    ACT(rstd[:, :], rstd[:, :], AF.Exp, [B_rstd], [B_rstd], scale=-0.5)
                    for c in range(KC):
                        eng = "dve" if c % 2 == 0 else "pool"
                        STT(eng, uT[:, c, :], hT[:, hb, c, :], prm[:, gcol + c:gcol + c + 1], rstd[:, :], ALU.mult, ALU.mult,
                            [B_hT[hb][c], B_prm, B_rstd], [B_uT[c]])

                rdflat = rd[:].rearrange("p a b -> p (a b)")

                def load_x_tile(tt, hbt):
                    for bb in range(NBT):
                        b = tt * NBT + bb
                        if b == 0:
                            DMA(rdflat[0:16, :], meta, [], B_rd)
                            DMA(rdflat[16:128, :], x[0:112, :], [], B_rd)
                        elif b == NB - 1:
                            P.add("pool", lambda e: e.memset(rdflat[:, :], 0.0), writes=B_rd)
                            DMA(rdflat[0:16, :], x[SEQ - 16:SEQ, :], [], B_rd)
                        else:
                            DMA(rdflat[:, :], x[128 * b - 16:128 * b + 112, :], [], B_rd)
                        for half in range(2):
                            pb = ps_next()
                            for cc in range(4):
                                c = half * 4 + cc
                                P.add("pe", (lambda e, pb=pb, cc=cc, c=c: e.transpose(psum[:, pb, cc * 128:(cc + 1) * 128], rdflat[:, c * 128:(c + 1) * 128], identf[:])),
                                      reads=B_rd + [B_const], writes=[PB[pb]])
                            CP("act" if half else "dve", hT[:, hbt, half * 4:half * 4 + 4, bb * 128:(bb + 1) * 128],
                               psum[:, pb, :].rearrange("p (c t) -> p c t", c=4), [PB[pb]], B_hT[hbt][half * 4:half * 4 + 4])

                for l in range(n_layers):
                    last_layer = (l == n_layers - 1)
                    DMA(lruw[:, :], lrubf[l], [B_lrubf], [B_lruw])
                    P.add("pool", lambda e: e.memset(rrun[:], 0.0), writes=[B_rrun])
                    P.add("pool", lambda e: e.memset(hstate[:], 0.0), writes=[B_hst8])
                    for j in range(4):
                        P.add("pool", (lambda e, j=j: e.memset(xcT[:, j, 0:3], 0.0)), writes=[B_xc[j]])
                    if l == 0:
                        load_x_tile(0, 0)
                    else:
                        DMA(hT[:, 0, :, :], hbuf[:, :, 0:NT], [B_hb[0]], B_hT[0])
                    for ti in range(n_tiles):
                        hb = ti % 2
                        b0 = ti * NBT
                        t0 = ti * NT
                        if ti + 1 < n_tiles:
                            if l == 0:
                                load_x_tile(ti + 1, 1 - hb)
                            else:
                                DMA(hT[:, 1 - hb, :, :], hbuf[:, :, t0 + NT:t0 + 2 * NT], [B_hb[ti + 1]], B_hT[1 - hb])
                        if l + 1 < n_layers:
                            if ti == 0:
                                conv_next = conv_dmas(l + 1)
                            per = -(-len(conv_next) // n_tiles)
                            for item in conv_next[ti * per:(ti + 1) * per]:
                                rec_conv(item)
                        rmsnorm_to_uT(hb, PR_NMIX + l * 8)
                        nkb = b0 + NBT

                        def proj_fm(slot, off, kstride, nch, evac, ring="A"):
                            for j in range(nch):
                                pb = ps_next(ring)
                                for k in range(KC):
                                    MM(psum[:, pb, 0:NT], wv(slot, off, kstride, k, j * 128, 128), uT[:, k, :], k == 0, k == KC - 1,
                                       [B_wr[slot], B_uT[k]], [PB[pb]])
                                evac(j, pb)
                                yield

                        def run(gen):
                            for _ in gen:
                                pass

                        s5 = use_slab(l, ti, "xc")
                        run(proj_fm(s5, 0, 512, 4, lambda j, pb: CP("act", xcT[:, j, 3:3 + NT], psum[:, pb, 0:NT], [PB[pb]], [B_xc[j]])))
                        s6 = use_slab(l, ti, "yc")

                        def gelu_evac(j, pb):
                            ACT(R[:, 20 + j, :], psum[:, pb, 0:NT], AF.Gelu_apprx_tanh, [PB[pb]], [B_R[20 + j]])

                        run(proj_fm(s6, 0, 512, 4, gelu_evac))
                        def parta_gen():
                            s2 = use_slab(l, ti, "qf")
                            yield from proj_fm(s2, 0, 512, 4, lambda j, pb: ACT(R[:, 4 + j, :], psum[:, pb, 0:NT], AF.Identity, [PB[pb]], [B_R[4 + j]], scale=0.125), ring="F")
                            s3 = use_slab(l, ti, "kf")
                            yield from proj_fm(s3, 0, 512, 4, lambda j, pb: CP("dve", kTf[:, j, t0:t0 + NT], psum[:, pb, 0:NT], [PB[pb]], [B_kTf[j][ti]]), ring="F")
                            s4 = use_slab(l, ti, "vf")
                            for bb in range(NBT):
                                pb = ps_next("F")
                                for k in range(KC):
                                    MM(psum[:, pb, 0:512], uT[:, k, bb * 128:(bb + 1) * 128], wv(s4, 0, 512, k, 0, 512), k == 0, k == KC - 1,
                                       [B_wr[s4], B_uT[k]], [PB[pb]])
                                P.add("dve" if bb % 2 == 0 else "act", (lambda e, pb=pb, bb=bb, b0=b0, isact=(bb % 2 == 1): (
                                    e.activation(out=bass.AP(vfx, (b0 + bb) * VW, [[NB * VW, 128], [192, 4], [128, 2], [1, 64]]),
                                                 in_=psum[:, pb, 0:512].rearrange("p (c g d) -> p c g d", c=4, g=2), func=AF.Copy) if isact else
                                    e.tensor_copy(out=bass.AP(vfx, (b0 + bb) * VW, [[NB * VW, 128], [192, 4], [128, 2], [1, 64]]),
                                                  in_=psum[:, pb, 0:512].rearrange("p (c g d) -> p c g d", c=4, g=2)))),
                                    reads=[PB[pb]], writes=[B_vf[b0 + bb]])
                                yield
                            s1 = use_slab(l, ti, "kav")
                            if ti > 0:
                                CP("pool", kTa[:, 0:128], kTa[:, NT:NT + 128], [B_kTa], [B_kTa])
                                CP("pool", vsw[:, 0, :], vsw[:, NBT, :], [B_vsw[NBT]], [B_vsw[0]])
                            yield from proj_fm(s1, 0, 264, 1, lambda j, pb: CP("dve", kTa[:, 128:128 + NT], psum[:, pb, 0:NT], [PB[pb]], [B_kTa]), ring="F")
                            for bb in range(NBT):
                                pb = ps_next("F")
                                for k in range(KC):
                                    MM(psum[:, pb, 0:128], uT[:, k, bb * 128:(bb + 1) * 128], wv(s1, 0, 264, k, 128, 128), k == 0, k == KC - 1,
                                       [B_wr[s1], B_uT[k]], [PB[pb]])
                                P.add("dve", (lambda e, pb=pb, bb=bb: e.tensor_copy(
                                    out=bass.AP(vsw, (bb + 1) * 320 + 64, [[(NBT + 1) * 320, 128], [128, 2], [1, 64]]),
                                    in_=psum[:, pb, 0:128].rearrange("p (g d) -> p g d", g=2))),
                                    reads=[PB[pb]], writes=[B_vsw[bb + 1]])
                                pb2 = ps_next("F")
                                for k in range(KC):
                                    MM(psum[:, pb2, 0:8], uT[:, k, bb * 128:(bb + 1) * 128], wv(s1, 0, 264, k, 256, 8), k == 0, k == KC - 1,
                                       [B_wr[s1], B_uT[k]], [PB[pb2]])
                                q = bb % 4
                                TT("dve", sm[:, q, :], psum[:, pb2, 0:8], fb[:, l * 8:(l + 1) * 8], ALU.add, [PB[pb2], B_prm], [B_sm[q]])
                                ACT(sm[:, q, :], sm[:, q, :], AF.Exp, [B_sm[q]], [B_sm[q]], scale=-1.0)
                                ACT(sm[:, q, :], sm[:, q, :], AF.Ln, [B_sm[q]], [B_sm[q]], bias=1.0)
                                pc_ = ps_next("F")
                                MM(psum[:, pc_, 0:8], trif[:, :], sm[:, q, :], True, True, [B_const, B_sm[q]], [PB[pc_]])
                                MM(psum[:, pc_, 8:16], onesf[:, :], sm[:, q, :], True, True, [B_const, B_sm[q]], [PB[pc_]])
                                if bb == 0:
                                    CP("dve", nref[:, :], rrun[:, :], [B_rrun], [B_nref])
                                TT("dve", negcum[:, b0 + bb, :], psum[:, pc_, 0:8], rrun[:, :], ALU.add, [PB[pc_], B_rrun], [B_negcum])
                                TT("dve", rrun[:, :], psum[:, pc_, 8:16], rrun[:, :], ALU.add, [PB[pc_], B_rrun], [B_rrun])
                                yield
                            TT("dve", nref[:, :], nref[:, :], rrun[:, :], ALU.add, [B_nref, B_rrun], [B_nref])
                            TS("dve", nref[:, :], nref[:, :], 0.5, None, ALU.mult, None, [B_nref], [B_nref])
                            P.add("dve", (lambda e, nkb=nkb: e.tensor_tensor(
                                out=btab[:, 0:nkb, :], in0=negcum[:, 0:nkb, :],
                                in1=bass.AP(nref, 0, [[8, 128], [0, nkb], [1, 8]]), op=ALU.subtract)),
                                reads=[B_negcum, B_nref], writes=[B_btab])

                            s0 = use_slab(l, ti, "qa")
                            yield from proj_fm(s0, 0, 512, 4, lambda j, pb: CP("dve", R[:, j, :], psum[:, pb, 0:NT], [PB[pb]], [B_R[j]]), ring="F")

                        def lru_gen():
                            for j in range(4):
                                pc0 = PR_CW + l * 16
                                cw = lambda t, j=j: prm[:, pc0 + t * 4 + j:pc0 + t * 4 + j + 1]
                                cbj = prm[:, PR_CB + l * 4 + j:PR_CB + l * 4 + j + 1]
                                TS("dve", W[:, 0, :], xcT[:, j, 3:3 + NT], cw(3), cbj, ALU.mult, ALU.add, [B_xc[j], B_prm], [B_W[0]])
                                for t in range(3):
                                    STT("dve", W[:, 0, :], xcT[:, j, t:t + NT], cw(t), W[:, 0, :], ALU.mult, ALU.add, [B_xc[j], B_prm, B_W[0]], [B_W[0]])
                                CP("pool", xcT[:, j, 0:3], xcT[:, j, NT:NT + 3], [B_xc[j]], [B_xc[j]])
                                qq = j % 2
                                CP("dve", cvb[:, qq, :], W[:, 0, :], [B_W[0]], [B_cvb[qq]])
                                yield
                                pr = ps_next("G")
                                MM(psum[:, pr, 0:NT], lruw[:, j * 128:(j + 1) * 128], cvb[:, qq, :], True, True, [B_lruw, B_cvb[qq]], [PB[pr]])
                                pi_ = ps_next("G")
                                MM(psum[:, pi_, 0:NT], lruw[:, (4 + j) * 128:(5 + j) * 128], cvb[:, qq, :], True, True, [B_lruw, B_cvb[qq]], [PB[pi_]])
                                ACT(W[:, 1, :], psum[:, pr, 0:NT], AF.Sigmoid, [PB[pr], B_prm], [B_W[1]], bias=prm[:, PR_BR + l * 4 + j:PR_BR + l * 4 + j + 1])
                                ACT(W[:, 2, :], psum[:, pi_, 0:NT], AF.Sigmoid, [PB[pi_], B_prm], [B_W[2]], bias=prm[:, PR_BI + l * 4 + j:PR_BI + l * 4 + j + 1])
                                yield
                                ACT(W[:, 3, :], W[:, 1, :], AF.Exp, [B_W[1], B_prm], [B_W[3]], scale=ls8[:, l * 4 + j:l * 4 + j + 1])
                                ACT(W[:, 4, :], W[:, 1, :], AF.Exp, [B_W[1], B_prm], [B_W[4]], scale=ls16[:, l * 4 + j:l * 4 + j + 1])
                                ACT(W[:, 4, :], W[:, 4, :], AF.Ln, [B_W[4]], [B_W[4]], bias=1.0, scale=-(1.0 - 1e-7))
                                ACT(W[:, 4, :], W[:, 4, :], AF.Exp, [B_W[4]], [B_W[4]], scale=0.5)
                                TT("dve", W[:, 2, :], W[:, 2, :], W[:, 0, :], ALU.mult, [B_W[2], B_W[0]], [B_W[2]])
                                TT("dve", W[:, 2, :], W[:, 2, :], W[:, 4, :], ALU.mult, [B_W[2], B_W[4]], [B_W[2]])
                                P.add("dve", (lambda e, j=j: e.tensor_tensor_scan(out=W[:, 1, :], data0=W[:, 3, :], data1=W[:, 2, :],
                                                                                   initial=hstate[:, j:j + 1], op0=ALU.mult, op1=ALU.add)),
                                      reads=[B_W[3], B_W[2], B_hst8], writes=[B_W[1]])
                                CP("dve", hstate[:, j:j + 1], W[:, 1, NT - 1:NT], [B_W[1]], [B_hst8])
                                TT("dve", R[:, 16 + j, :], W[:, 1, :], R[:, 20 + j, :], ALU.mult, [B_W[1], B_R[20 + j]], [B_R[16 + j]])
                                yield

                        def interleave(mg, fg_, ratio):
                            mdone = fdone_ = False
                            cnt_ = 0
                            while not (mdone and fdone_):
                                if not mdone:
                                    try:
                                        next(mg)
                                    except StopIteration:
                                        mdone = True
                                cnt_ += 1
                                if not fdone_ and (mdone or cnt_ % ratio == 0):
                                    try:
                                        next(fg_)
                                    except StopIteration:
                                        fdone_ = True

                        interleave(parta_gen(), lru_gen(), 1)
                        def fox_gen():
                            for c in range(4):
                                accs = (6, 7)
                                pend = []

                                def pv_pair(item):
                                    kb_, qs_, n_, pis = item
                                    for hh in range(2):
                                        vo = c * 192 + (0 if hh == 0 else 64)
                                        MM(psum[:, accs[hh], qs_:NT], vfx[:, kb_, vo:vo + 128], pT[:, pis[hh], 0:n_], kb_ == 0, kb_ == nkb - 1,
                                           [B_vf[kb_], B_vones, B_pT[pis[hh]]], [PB[accs[hh]]])

                                for kb in range(nkb):
                                    qs = max(0, kb - b0) * 128
                                    n = NT - qs
                                    pbs = (ps_next("F"), ps_next("F"))
                                    for hh in range(2):
                                        po = hh * 64
                                        MM(psum[:, pbs[hh], 0:n], kTf[po:po + 64, c, kb * 128:(kb + 1) * 128], R[po:po + 64, 4 + c, qs:NT], True, True,
                                           [B_kTf[c][kb // NBT], B_R[4 + c]], [PB[pbs[hh]]])
                                    pis = (pt_next(), pt_next())
                                    for hh in range(2):
                                        h = 2 * c + hh
                                        ACT(pT[:, pis[hh], 0:n], psum[:, pbs[hh], 0:n], AF.Exp, [PB[pbs[hh]], B_btab], [B_pT[pis[hh]]], bias=btab[:, kb, h:h + 1])
                                        if kb >= b0:
                                            TT("pool", pT[:, pis[hh], 0:128], pT[:, pis[hh], 0:128], trib[:, :], ALU.mult, [B_pT[pis[hh]], B_const], [B_pT[pis[hh]]])
                                    pend.append((kb, qs, n, pis))
                                    if len(pend) > 1:
                                        pv_pair(pend.pop(0))
                                    yield
                                while pend:
                                    pv_pair(pend.pop(0))
                                for hh in range(2):
                                    q = hh
                                    dlo, olo = (64, 0) if hh == 0 else (0, 64)
                                    RECIP(rd[dlo:dlo + 64, q, 0:NT], psum[dlo:dlo + 64, accs[hh], 0:NT], [PB[accs[hh]]], [B_rd[q]])
                                    TT("dve", R[olo:olo + 64, 12 + c, :], psum[olo:olo + 64, accs[hh], 0:NT], rd[dlo:dlo + 64, q, 0:NT], ALU.mult,
                                       [PB[accs[hh]], B_rd[q]], [B_R[12 + c]])
                                yield

                        def filler_gen():
                            SL2HH = [0, 2, 1, 3]
                            for bb in range(NBT):
                                b = b0 + bb
                                qc0 = bb * 128
                                for g in range(2):
                                    acc = 4
                                    kinds = [1] if b == 0 else [0, 1]
                                    pvl = []
                                    for kidx, kind in enumerate(kinds):
                                        pb = 5
                                        kc0 = bb * 128 if kind == 0 else (bb + 1) * 128
                                        for slot in range(4):
                                            hh = SL2HH[slot]
                                            MM(psum[:, pb, slot * 128:(slot + 1) * 128], kTa[g * 64:(g + 1) * 64, kc0:kc0 + 128],
                                               R[g * 64:(g + 1) * 64, hh, qc0:qc0 + 128], True, True, [B_kTa, B_R[hh]], [PB[pb]])
                                        STT("dve", swt[:, 0, :], psum[:, pb, :], 0.125, swab[:, g * 2 + kind, :], ALU.mult, ALU.add,
                                            [PB[pb], B_swab], [B_swt[0]])
                                        pi = 4 + (st["pts"] % 2)
                                        st["pts"] += 1
                                        ACT(pT[:, pi, :], swt[:, 0, :], AF.Exp, [B_swt[0]], [B_pT[pi]])
                                        vs = bb if kind == 0 else bb + 1
                                        pvl.append((vs, pi))
                                    yield
                                    for kidx, (vs, pi) in enumerate(pvl):
                                        MM(psum[:, acc, 0:256], vsw[:, vs, 64 + g * 128:192 + g * 128], pT[:, pi, 0:256], kidx == 0, kidx == len(pvl) - 1,
                                           [B_vsw[vs], B_pT[pi]], [PB[acc]])
                                    for kidx, (vs, pi) in enumerate(pvl):
                                        MM(psum[:, acc, 256:512], vsw[:, vs, g * 128:128 + g * 128], pT[:, pi, 256:512], kidx == 0, kidx == len(pvl) - 1,
                                           [B_vsw[vs], B_pT[pi]], [PB[acc]])
                                    q = g
                                    for slot in range(4):
                                        h = 4 * g + SL2HH[slot]
                                        cs = slice(slot * 128, (slot + 1) * 128)
                                        dlo, olo = (64, 0) if slot < 2 else (0, 64)
                                        TS("dve", rd[dlo:dlo + 64, q, cs], psum[dlo:dlo + 64, acc, cs], es[dlo:dlo + 64, l * 8 + h:l * 8 + h + 1], None,
                                           ALU.add, None, [PB[acc], B_prm], [B_rd[q]])
                                    for half in range(2):
                                        dlo = 64 if half == 0 else 0
                                        cs2 = slice(half * 256, half * 256 + 256)
                                        RECIP(rd[dlo:dlo + 64, q, cs2], rd[dlo:dlo + 64, q, cs2], [B_rd[q]], [B_rd[q]])
                                    for slot in range(4):
                                        h = 4 * g + SL2HH[slot]
                                        cs = slice(slot * 128, (slot + 1) * 128)
                                        dlo, olo = (64, 0) if slot < 2 else (0, 64)
                                        ch = 8 + h // 2
                                        TT("dve", R[olo:olo + 64, ch, qc0:qc0 + 128], psum[olo:olo + 64, acc, cs], rd[dlo:dlo + 64, q, cs], ALU.mult,
                                           [PB[acc], B_rd[q]], [B_R[ch]])
                                    yield

                        n_fox = 4 * (nkb + 1)
                        n_fill = NBT * 4
                        fg, gg = fox_gen(), filler_gen()
                        ratio = max(1, n_fox // n_fill)
                        fdone = gdone = False
                        cnt = 0
                        while not (fdone and gdone):
                            if not fdone:
                                try:
                                    next(fg)
                                except StopIteration:
                                    fdone = True
                            cnt += 1
                            if not gdone and (fdone or cnt % ratio == 0):
                                try:
                                    next(gg)
                                except StopIteration:
                                    gdone = True

                        if dbg and l == 0 and ti == 0:
                            DMA(dbgo, R[:, 8:20, :], B_R[8:20], [])
                        for m in range(8):
                            sm_ = use_slab(l, ti, "mg%d" % m)
                            wo = 0 if m % 2 == 0 else 3
                            pg = []
                            for x_ in range(3):
                                pb = ps_next()
                                for k in range(KC):
                                    MM(psum[:, pb, 0:NT], wv(sm_, x_ * 128, 384, k, 0, 128), uT[:, k, :], k == 0, k == KC - 1,
                                       [B_wr[sm_], B_uT[k]], [PB[pb]])
                                ACT(W[:, wo + x_, :], psum[:, pb, 0:NT], AF.Sigmoid, [PB[pb]], [B_W[wo + x_]])
                            for x_ in range(3):
                                pb = ps_next()
                                for kk in range(4):
                                    MM(psum[:, pb, 0:NT], wv(sm_, 3072 + x_ * 128, 384, kk, 0, 128), R[:, 8 + 4 * x_ + kk, :], kk == 0, kk == 3,
                                       [B_wr[sm_], B_R[8 + 4 * x_ + kk]], [PB[pb]])
                                TT("dve", W[:, wo + x_, :], W[:, wo + x_, :], psum[:, pb, 0:NT], ALU.mult, [B_W[wo + x_], PB[pb]], [B_W[wo + x_]])
                            TT("pool", W[:, wo, :], W[:, wo, :], W[:, wo + 1, :], ALU.add, [B_W[wo], B_W[wo + 1]], [B_W[wo]])
                            TT("pool", R[:, m, :], W[:, wo, :], W[:, wo + 2, :], ALU.add, [B_W[wo], B_W[wo + 2]], [B_R[m]])
                        for hf in range(2):
                            so_ = use_slab(l, ti, "wo%d" % hf)
                            for mm in range(4):
                                m = hf * 4 + mm
                                pb = ps_next()
                                for k in range(KC):
                                    MM(psum[:, pb, 0:NT], wv(so_, 0, 512, k, mm * 128, 128), R[:, k, :], k == 0, k == KC - 1,
                                       [B_wr[so_], B_R[k]], [PB[pb]])
                                TT("dve", hT[:, hb, m, :], hT[:, hb, m, :], psum[:, pb, 0:NT], ALU.add, [B_hT[hb][m], PB[pb]], [B_hT[hb][m]])
                        rmsnorm_to_uT(hb, PR_NFFN + l * 8)
                        for s_ in range(11):
                            sf = use_slab(l, ti, "fi%d" % s_)
                            for pp in range(2):
                                j = 2 * s_ + pp
                                pbg = ps_next()
                                for k in range(KC):
                                    MM(psum[:, pbg, 0:NT], wv(sf, 0, 512, k, (2 * pp) * 128, 128), uT[:, k, :], k == 0, k == KC - 1,
                                       [B_wr[sf], B_uT[k]], [PB[pbg]])
                                pbu = ps_next()
                                for k in range(KC):
                                    MM(psum[:, pbu, 0:NT], wv(sf, 0, 512, k, (2 * pp + 1) * 128, 128), uT[:, k, :], k == 0, k == KC - 1,
                                       [B_wr[sf], B_uT[k]], [PB[pbu]])
                                wq = j % 6
                                ACT(W[:, wq, :], psum[:, pbg, 0:NT], AF.Silu, [PB[pbg]], [B_W[wq]])
                                TT("dve", R[:, j, :], W[:, wq, :], psum[:, pbu, 0:NT], ALU.mult, [B_W[wq], PB[pbu]], [B_R[j]])
                        for m in range(8):
                            sf = use_slab(l, ti, "fo%d" % m)
                            pb = ps_next()
                            for k in range(FK):
                                MM(psum[:, pb, 0:NT], wv(sf, 0, 128, k, 0, 128), R[:, k, :], k == 0, k == FK - 1,
                                   [B_wr[sf], B_R[k]], [PB[pb]])
                            TT("dve", hT[:, hb, m, :], hT[:, hb, m, :], psum[:, pb, 0:NT], ALU.add, [B_hT[hb][m], PB[pb]], [B_hT[hb][m]])
                        if not last_layer:
                            DMA(hbuf[:, :, t0:t0 + NT], hT[:, hb, :, :], B_hT[hb], [B_hb[ti]])
                        else:
                            ssb = ps_next()
                            for c in range(KC):
                                q = c % 2
                                ACT(sq[:, q, :], hT[:, hb, c, :], AF.Square, [B_hT[hb][c]], [B_sq[q]])
                                MM(psum[:, ssb, 0:NT], onesb[:, :], sq[:, q, :], c == 0, c == KC - 1, [B_const, B_sq[q]], [PB[ssb]])
                            ACT(rstd[:, :], psum[:, ssb, 0:NT], AF.Ln, [PB[ssb]], [B_rstd], bias=epsb[:, 0:1], scale=1.0 / D)
                            ACT(rstd[:, :], rstd[:, :], AF.Exp, [B_rstd], [B_rstd], scale=-0.5)
                            for c in range(KC):
                                eng = "dve" if c % 2 == 0 else "pool"
                                STT(eng, hT[:, hb, c, :], hT[:, hb, c, :], prm[:, PR_NFIN + c:PR_NFIN + c + 1], rstd[:, :], ALU.mult, ALU.mult,
                                    [B_hT[hb][c], B_prm, B_rstd], [B_hT[hb][c]])
                            for bb in range(NBT):
                                b = b0 + bb
                                oq = b % 2
                                for half in range(2):
                                    pb = ps_next()
                                    for cc in range(4):
                                        c = half * 4 + cc
                                        P.add("pe", (lambda e, pb=pb, cc=cc, c=c, bb=bb, hb=hb: e.transpose(
                                            psum[:, pb, cc * 128:(cc + 1) * 128], hT[:, hb, c, bb * 128:(bb + 1) * 128], identf[:])),
                                            reads=[B_hT[hb][c], B_const], writes=[PB[pb]])
                                    CP("act" if half else "dve", ost[oq][:, half * 512:(half + 1) * 512], psum[:, pb, :], [PB[pb]], B_ost[oq])
                                if b == 0:
                                    DMA(out[0:112, :], ost[oq][16:128, :], B_ost[oq], [])
                                elif b == NB - 1:
                                    DMA(out[SEQ - 16:SEQ, :], ost[oq][0:16, :], B_ost[oq], [])
                                else:
                                    DMA(out[128 * b - 16:128 * b + 112, :], ost[oq][:, :], B_ost[oq], [])
                fin = P.add("sp", None, reads=[], writes=[])
                fin.deps = [o for e_ in ENGS for o in P.ops[e_] if o.is_dma]
                P.emit(nc, sems, dsems, base)
    return nc


def make_consts():
    c = np.zeros((128, 384), np.float32)
    c[:, 0:128] = np.eye(128, dtype=np.float32)
    c[:, 128:256] = np.triu(np.ones((128, 128), np.float32))
    import math
    for dist in range(128):
        if dist < 16:
            b = dist
        else:
            sc = np.log(np.float32(max(dist, 1)) / np.float32(16)) / np.float32(math.log(128 / 16))
            b = min(16 + int(np.float32(sc) * np.float32(16)), 31)
        c[b, 256 + dist] = 1.0
    return c


_NC_CACHE = {}


def kernel(**inputs):
    if "nc" not in _NC_CACHE:
        _NC_CACHE["nc"] = build()
    nc = _NC_CACHE["nc"]
    cst = make_consts()
    xs = np.ascontiguousarray(inputs["x"], dtype=np.float32)
    shared = {k: np.ascontiguousarray(v, dtype=np.float32) for k, v in inputs.items() if k != "x"}
    shared["consts"] = cst
    in_maps = []
    for i in range(8):
        m = dict(shared)
        m["x"] = xs[i]
        in_maps.append(m)
    res = run_bass_kernel_spmd(nc, in_maps, core_ids=list(range(8)))
    return np.stack([np.asarray(r["out"], dtype=np.float32) for r in res.results], axis=0)
```
